# Optimizing a Trainium2 kernel written in Bass

```python
import jax, jax.numpy as jnp
from jax import lax
import numpy as np

D_MODEL = 1024
BATCH = 16
SEQ = 256
DEPTH = 4
DEC_BATCH = 4
DEC_SEQ = 2048
PAST_LEN = 256

GRID_W = 64
D_RNN = 1024
N_LRU_HEADS = 4
LRU_HEAD_DIM = D_RNN // N_LRU_HEADS
LRU_C = 8.0
CONV_W = 4
CONV_LEFT = 2
CONV_RIGHT = 1
D_FNET = 512
N_FNET_GROUPS = 8
FNET_GROUP_DIM = D_FNET // N_FNET_GROUPS
N_IN = 2 * D_RNN + D_FNET + 2 * D_MODEL
D_FF = ((8 * D_MODEL + 3 * 256 - 1) // (3 * 256)) * 256
EPS = 1e-6

kernel_name = "hybrid_rglru_fnet_diffusion_step"


def rmsnorm(x, g):
    xf = x.astype(jnp.float32)
    y = xf * lax.rsqrt(jnp.mean(xf * xf, axis=-1, keepdims=True) + EPS)
    return (y * g.astype(jnp.float32)).astype(x.dtype)


def adaln_params(c_silu, ada_w, ada_b):
    mod = c_silu @ ada_w + ada_b
    return [m[:, None, :] for m in jnp.split(mod, 6, axis=-1)]


def depthwise_conv(x, w, b):
    L = x.shape[1]
    xp = jnp.pad(x, ((0, 0), (CONV_LEFT, CONV_RIGHT), (0, 0)))
    y = b
    for k in range(CONV_W):
        y = y + xp[:, k:k + L, :] * w[k]
    return y


def _lin_combine(e1, e2):
    a1, b1 = e1
    a2, b2 = e2
    return a1 * a2, a2 * b1 + b2


def rglru_direction(x, wa, ba, wx, bx, lam, h0, reverse):
    B, L, _ = x.shape
    xh = x.reshape(B, L, N_LRU_HEADS, LRU_HEAD_DIM)
    r = jax.nn.sigmoid(jnp.einsum('blhd,hde->blhe', xh, wa).reshape(B, L, D_RNN) + ba)
    i = jax.nn.sigmoid(jnp.einsum('blhd,hde->blhe', xh, wx).reshape(B, L, D_RNN) + bx)
    log_a = -LRU_C * r * jax.nn.softplus(-lam)
    a = jnp.exp(log_a)
    mult = jnp.sqrt(-jnp.expm1(2.0 * log_a))
    b = mult * (i * x)
    if reverse:
        b = b.at[:, -1].add(a[:, -1] * h0)
    else:
        b = b.at[:, 0].add(a[:, 0] * h0)
    _, h = lax.associative_scan(_lin_combine, (a, b), axis=1, reverse=reverse)
    return h


def rglru_bidir(x, wa, ba, wx, bx, lam, h0_f, h0_b):
    xf = x.astype(jnp.float32)
    h_f = rglru_direction(xf, wa[0].astype(jnp.float32), ba[0].astype(jnp.float32), wx[0].astype(jnp.float32),
                          bx[0].astype(jnp.float32), lam[0].astype(jnp.float32), h0_f.astype(jnp.float32), False)
    h_b = rglru_direction(xf, wa[1].astype(jnp.float32), ba[1].astype(jnp.float32), wx[1].astype(jnp.float32),
                          bx[1].astype(jnp.float32), lam[1].astype(jnp.float32), h0_b.astype(jnp.float32), True)
    out = (h_f + h_b).astype(x.dtype)
    return out, h_f[:, -1].astype(x.dtype), h_b[:, 0].astype(x.dtype)


def fourier_mix(xf, on_grid):
    B, L, _ = xf.shape
    xg = xf.astype(jnp.float32).reshape(B, L, N_FNET_GROUPS, FNET_GROUP_DIM)
    if on_grid:
        rows = L // GRID_W
        xg = xg.reshape(B, rows, GRID_W, N_FNET_GROUPS, FNET_GROUP_DIM)
        f = jnp.fft.fftn(xg, axes=(1, 2, 4), norm="ortho").real
    else:
        f = jnp.fft.fftn(xg, axes=(1, 3), norm="ortho").real
    return f.reshape(B, L, D_FNET).astype(xf.dtype)


def block(x, mod, h0_f, h0_b, on_grid, norm1_g, norm2_g, w_in, b_in, conv_w, conv_b, lru_wa, lru_ba, lru_wx,
          lru_bx, lru_lambda, w_lru_out, w_fnet_out, w_out, ffn_w_in, ffn_w_out):
    shift1, scale1, gate1, shift2, scale2, gate2 = mod
    h = rmsnorm(x, norm1_g) * (1.0 + scale1) + shift1
    z = h @ w_in + b_in
    x_r = z[..., :D_RNN]
    y_r = z[..., D_RNN:2 * D_RNN]
    x_f = z[..., 2 * D_RNN:2 * D_RNN + D_FNET]
    g = jax.nn.sigmoid(z[..., 2 * D_RNN + D_FNET:])
    g_a = g[..., :D_MODEL]
    g_b = g[..., D_MODEL:]
    x_r = depthwise_conv(x_r, conv_w, conv_b)
    rec, hf_last, hb_first = rglru_bidir(x_r, lru_wa, lru_ba, lru_wx, lru_bx, lru_lambda, h0_f, h0_b)
    out_a = (rec * jax.nn.gelu(y_r)) @ w_lru_out
    out_b = fourier_mix(x_f, on_grid) @ w_fnet_out
    mixed = (g_a * out_a + g_b * out_b) @ w_out
    x = x + gate1 * mixed
    h2 = rmsnorm(x, norm2_g) * (1.0 + scale2) + shift2
    uv = h2 @ ffn_w_in
    u = uv[..., :D_FF]
    v = uv[..., D_FF:]
    x = x + gate2 * ((jax.nn.silu(u) * v) @ ffn_w_out)
    return x, hf_last, hb_first


def setup_inputs(seed: int = 0) -> dict:
    key = jax.random.key(seed)
    ks = jax.random.split(key, 26)
    f32 = jnp.float32
    nrm = lambda k, s, sc: jax.random.normal(k, s, f32) * sc
    u = jax.random.uniform(ks[15], (DEPTH, 2, D_RNN), f32, minval=0.9, maxval=0.999)
    a0 = u ** (1.0 / LRU_C)
    lru_lambda = jnp.log(a0) - jnp.log1p(-a0)
    return {
        "x_prompt": nrm(ks[0], (BATCH, SEQ, D_MODEL), 1.0),
        "x_sample": nrm(ks[1], (DEC_BATCH, DEC_SEQ, D_MODEL), 1.0),
        "state_lru": nrm(ks[2], (DEC_BATCH, DEPTH, 2, D_RNN), 0.5),
        "c": nrm(ks[3], (DEC_BATCH, D_MODEL), 1.0),
        "c_ctx": nrm(ks[4], (D_MODEL,), 1.0),
        "norm1_g": 1.0 + nrm(ks[5], (DEPTH, D_MODEL), 0.02),
        "norm2_g": 1.0 + nrm(ks[6], (DEPTH, D_MODEL), 0.02),
        "ada_w": nrm(ks[7], (DEPTH, D_MODEL, 6 * D_MODEL), 0.02),
        "ada_b": nrm(ks[8], (DEPTH, 6 * D_MODEL), 0.02),
        "w_in": nrm(ks[9], (DEPTH, D_MODEL, N_IN), D_MODEL ** -0.5),
        "b_in": nrm(ks[10], (DEPTH, N_IN), 0.02),
        "conv_w": nrm(ks[11], (DEPTH, CONV_W, D_RNN), CONV_W ** -0.5),
        "conv_b": nrm(ks[12], (DEPTH, D_RNN), 0.02),
        "lru_wa": nrm(ks[13], (DEPTH, 2, N_LRU_HEADS, LRU_HEAD_DIM, LRU_HEAD_DIM), LRU_HEAD_DIM ** -0.5),
        "lru_ba": nrm(ks[14], (DEPTH, 2, D_RNN), 0.02),
        "lru_wx": nrm(ks[16], (DEPTH, 2, N_LRU_HEADS, LRU_HEAD_DIM, LRU_HEAD_DIM), LRU_HEAD_DIM ** -0.5),
        "lru_bx": nrm(ks[17], (DEPTH, 2, D_RNN), 0.02),
        "lru_lambda": lru_lambda,
        "w_lru_out": nrm(ks[18], (DEPTH, D_RNN, D_MODEL), D_RNN ** -0.5),
        "w_fnet_out": nrm(ks[19], (DEPTH, D_FNET, D_MODEL), D_FNET ** -0.5),
        "w_out": nrm(ks[20], (DEPTH, D_MODEL, D_MODEL), D_MODEL ** -0.5),
        "ffn_w_in": nrm(ks[21], (DEPTH, D_MODEL, 2 * D_FF), D_MODEL ** -0.5),
        "ffn_w_out": nrm(ks[22], (DEPTH, D_FF, D_MODEL), D_FF ** -0.5),
        "final_g": 1.0 + nrm(ks[23], (D_MODEL,), 0.02),
    }


def reference(x_prompt, x_sample, state_lru, c, c_ctx, norm1_g, norm2_g, ada_w, ada_b, w_in, b_in, conv_w, conv_b,
              lru_wa, lru_ba, lru_wx, lru_bx, lru_lambda, w_lru_out, w_fnet_out, w_out, ffn_w_in, ffn_w_out,
              final_g):
    xp = x_prompt
    xs = x_sample
    B = x_prompt.shape[0]
    zeros = jnp.zeros((B, D_RNN), x_prompt.dtype)
    c_ctx_silu = jax.nn.silu(c_ctx)[None, :]
    c_silu = jax.nn.silu(c)
    layer_states = []
    for l in range(DEPTH):
        lp = (norm1_g[l], norm2_g[l], w_in[l], b_in[l], conv_w[l], conv_b[l], lru_wa[l], lru_ba[l], lru_wx[l],
              lru_bx[l], lru_lambda[l], w_lru_out[l], w_fnet_out[l], w_out[l], ffn_w_in[l], ffn_w_out[l])
        mod_ctx = adaln_params(c_ctx_silu, ada_w[l], ada_b[l])
        mod_lat = adaln_params(c_silu, ada_w[l], ada_b[l])
        xp, hf_last, hb_first = block(xp, mod_ctx, zeros, zeros, False, *lp)
        layer_states.append(jnp.stack([hf_last, hb_first], axis=1))
        xs, _, _ = block(xs, mod_lat, state_lru[:, l, 0], state_lru[:, l, 1], True, *lp)
    new_state_lru = jnp.stack(layer_states, axis=1)
    y_prompt = rmsnorm(xp, final_g)
    y_sample = rmsnorm(xs, final_g)
    return (y_prompt, y_sample, new_state_lru)
```

```python
import os
import numpy as np
import ml_dtypes
import concourse.bass as bass
import concourse.mybir as mybir
from concourse.bass_utils import run_bass_kernel_spmd

F32 = mybir.dt.float32
BF16 = mybir.dt.bfloat16
AF = mybir.ActivationFunctionType
ALU = mybir.AluOpType

D = 1024
L = 4
T = 2048
NQ = 4
QW = 512
D_RNN = 1024
D_FNET = 512
N_IN = 4608
D_FF = 2816
NJ = 22
EPS = 1e-6
NSEG = 8
SEGW = 256

P_N1G, P_N2G, P_ADAB, P_BIN, P_CONVW, P_CONVB, P_BA, P_BX, P_LAM = 0, 8, 16, 64, 100, 132, 140, 156, 172
NP = 188


class Page:
    __slots__ = ("w", "rs")

    def __init__(self):
        self.w = None
        self.rs = []


class TT:
    __slots__ = ("ap", "pages")

    def __init__(self, ap, pages):
        self.ap = ap
        self.pages = pages


class Eng:
    def __init__(self, nc, e, name, is_pe=False):
        self.e = e
        self.name = name
        self.sem = nc.alloc_semaphore("sem_" + name)
        self.n = 0
        self.waited = {}
        self.is_pe = is_pe

    def wait(self, ev):
        if ev is None:
            return
        key, h, val, src = ev
        if self.is_pe and src is self:
            return
        if self.waited.get(key, 0) >= val:
            return
        self.e.wait_ge(h, val)
        self.waited[key] = val

    def deps(self, reads, writes):
        for t in reads:
            for p in t.pages:
                self.wait(p.w)
        for t in writes:
            for p in t.pages:
                self.wait(p.w)
                for r in p.rs:
                    self.wait(r)

    def done(self, ins, reads, writes, inc=True):
        if inc:
            self.n += 1
            ins.then_inc(self.sem, 1)
            ev = (self.name, self.sem, self.n, self)
        else:
            ev = (self.name, self.sem, self.n + 1, self)
        for t in reads:
            for p in t.pages:
                p.rs.append(ev)
        for t in writes:
            for p in t.pages:
                p.w = ev
                p.rs = []
        return ev


class Builder:
    def __init__(self, n_layers=L):
        self.n_layers = n_layers
        nc = bass.Bass("TRN2", target_bir_lowering=False)
        self.nc = nc
        self.pe = Eng(nc, nc.tensor, "pe", is_pe=True)
        self.act = Eng(nc, nc.scalar, "act")
        self.dve = Eng(nc, nc.vector, "dve")
        self.pool = Eng(nc, nc.gpsimd, "pool")
        self.sp = Eng(nc, nc.sync, "sp")
        self.nsem = 0
        self.bank_rr = 0

    def dram_in(self, name, shape, dt=F32):
        return self.nc.dram_tensor(name, list(shape), dt, kind="ExternalInput").ap()

    def dram_out(self, name, shape, dt=F32):
        return self.nc.dram_tensor(name, list(shape), dt, kind="ExternalOutput").ap()

    def small(self, name, shape, dt=F32):
        t = self.nc.alloc_sbuf_tensor(name, list(shape), dt)
        pg = Page()
        return t, pg

    def new_dma_sem(self, name):
        self.nsem += 1
        return [self.nc.alloc_semaphore("dsem_%s_%d" % (name, self.nsem)), 0, "d%s%d" % (name, self.nsem)]

    def dma(self, q, out_t, in_t, dsem, out_is_sbuf=True):
        q.deps([in_t], [out_t])
        ins = q.e.dma_start(out=out_t.ap, in_=in_t.ap)
        dsem[1] += 16
        ins.then_inc(dsem[0], 16)
        ev = (dsem[2], dsem[0], dsem[1], None)
        for p in in_t.pages:
            p.rs.append(ev)
        for p in out_t.pages:
            p.w = ev
            p.rs = []
        return ev

    def mm(self, ps, lhsT, rhs, start, stop, force_inc=False):
        pe = self.pe
        pe.deps([lhsT, rhs], [ps] if start else [])
        ins = pe.e.matmul(ps.ap, lhsT=lhsT.ap, rhs=rhs.ap, start=start, stop=stop)
        if stop:
            pe.done(ins, [lhsT, rhs], [ps], inc=True)
        elif force_inc:
            pe.done(ins, [lhsT, rhs], [], inc=True)
        elif False:
            pe.done(ins, [lhsT, rhs], [ps], inc=True)
        else:
            pe.done(ins, [lhsT, rhs], [], inc=False)

    def activation(self, out, in_, func, bias=None, scale=None, extra=()):
        e = self.act
        reads = [in_] + [x for x in (bias, scale) if isinstance(x, TT)] + list(extra)
        e.deps(reads, [out])
        kw = {}
        if bias is not None:
            kw["bias"] = bias.ap if isinstance(bias, TT) else bias
        if scale is not None:
            kw["scale"] = scale.ap if isinstance(scale, TT) else scale
        ins = e.e.activation(out=out.ap, in_=in_.ap, func=func, **kw)
        e.done(ins, reads, [out])

    def v_tt(self, out, in0, in1, op, eng=None):
        e = eng or self.dve
        e.deps([in0, in1], [out])
        ins = e.e.tensor_tensor(out=out.ap, in0=in0.ap, in1=in1.ap, op=op)
        e.done(ins, [in0, in1], [out])

    def v_stt(self, out, in0, scalar, in1, op0, op1):
        e = self.dve
        reads = [in0, in1] + ([scalar] if isinstance(scalar, TT) else [])
        e.deps(reads, [out])
        ins = e.e.scalar_tensor_tensor(out=out.ap, in0=in0.ap, scalar=scalar.ap if isinstance(scalar, TT) else scalar,
                                       in1=in1.ap, op0=op0, op1=op1)
        e.done(ins, reads, [out])

    def v_ts(self, out, in0, s1, s2, op0, op1=None, eng=None):
        e = eng or self.dve
        reads = [in0] + [x for x in (s1, s2) if isinstance(x, TT)]
        e.deps(reads, [out])
        a1 = s1.ap if isinstance(s1, TT) else s1
        a2 = s2.ap if isinstance(s2, TT) else s2
        if op1 is None:
            ins = e.e.tensor_scalar(out=out.ap, in0=in0.ap, scalar1=a1, scalar2=None, op0=op0)
        else:
            ins = e.e.tensor_scalar(out=out.ap, in0=in0.ap, scalar1=a1, scalar2=a2, op0=op0, op1=op1)
        e.done(ins, reads, [out])

    def v_copy(self, out, in_, eng=None):
        e = eng or self.dve
        e.deps([in_], [out])
        ins = e.e.tensor_copy(out=out.ap, in_=in_.ap)
        e.done(ins, [in_], [out])

    def v_recip(self, out, in_):
        e = self.dve
        e.deps([in_], [out])
        ins = e.e.reciprocal(out=out.ap, in_=in_.ap)
        e.done(ins, [in_], [out])

    def v_scan(self, out, d0, d1, initial=None):
        e = self.dve
        reads = [d0, d1] + ([initial] if initial is not None else [])
        e.deps(reads, [out])
        ins = e.e.tensor_tensor_scan(out=out.ap, data0=d0.ap, data1=d1.ap,
                                     initial=(initial.ap if initial is not None else 0.0), op0=ALU.mult, op1=ALU.add)
        e.done(ins, reads, [out])

    def build(self):
        nc = self.nc
        nl = self.n_layers
        xT = self.dram_in("xT", [D, T])
        cv = self.dram_in("cv", [128, 8])
        pl = self.dram_in("pl", [128, L * NP])
        fg = self.dram_in("fg", [128, 8])
        h0d = self.dram_in("h0", [128, L * 2 * 8])
        misc = self.dram_in("misc", [128, 33])
        awd = self.dram_in("aw", [L * 48, 128, 1024])
        mwd = self.dram_in("mw", [L * 8, 128, 3584])
        w_in = self.dram_in("w_in", [L, D, N_IN])
        lru_wa = self.dram_in("lru_wa", [L, 2, 4, 256, 256])
        lru_wx = self.dram_in("lru_wx", [L, 2, 4, 256, 256])
        w_out = self.dram_in("w_out", [L, D, D])
        ffn_w_in = self.dram_in("ffn_w_in", [L, D, 2 * D_FF])
        ffn_w_out = self.dram_in("ffn_w_out", [L, D_FF, D])
        csd = self.dram_in("cs", [128, 256], BF16)
        tabd = self.dram_in("tab", [32, 128, 2048], BF16)
        yT = self.dram_out("yT", [D, T])
        nsd = self.dram_out("ns", [128, L * 2 * 8 * NSEG])

        X = nc.alloc_sbuf_tensor("X", [128, 8, T], F32)
        xpg = [[Page() for _ in range(NQ)] for _ in range(8)]

        def xv(c, q):
            return TT(X[:, c, q * QW:(q + 1) * QW], [xpg[c][q]])

        cvt, cv_pg = self.small("cvt", [128, 8])
        csb, csb_pg = self.small("csb", [128, 8], BF16)
        plt, pl_pg = self.small("plt", [128, L * NP])
        fgt, fg_pg = self.small("fgt", [128, 8])
        h0t, h0_pg = self.small("h0t", [128, L * 2 * 8])
        misct, misc_pg = self.small("misct", [128, 33])
        der, der_pg = self.small("der", [128, L * 160])
        modt = [self.small("mod%d" % l, [128, 48]) for l in range(L)]
        sct = [self.small("sc%d" % l, [128, 16]) for l in range(L)]
        identf, ident_pg = self.small("identf", [128, 128])
        onesb, ones_pg = self.small("onesb", [128, 128], BF16)
        cst, cs_pg = self.small("cst", [128, 256], BF16)
        nst, ns_pg = self.small("nst", [128, L * 2 * 8 * NSEG])
        epst, eps_pg = self.small("epst", [128, 1])
        onet, one_pg = self.small("onet", [128, 1])
        q25t, q25_pg = self.small("q25t", [128, 1])
        diag = [self.small("diag%d" % i, [128, 7, 128], BF16) for i in range(1)]
        scr = [self.small("scr%d" % i, [128, QW]) for i in range(4)]
        scrb = [self.small("scrb%d" % i, [128, QW], BF16) for i in range(3)]
        tiny = [self.small("tiny%d" % i, [128, 8]) for i in range(4)]

        def sm(tp, lo=None, hi=None):
            t, pg = tp
            if lo is None:
                return TT(t[:], [pg])
            return TT(t[:, lo:hi], [pg])

        rem = nc.sbuf_bytes_remaining
        nslot = min(int(rem // 8192), 16)
        print('sbuf rem', rem, 'nslot', nslot)
        assert nslot >= 15, nslot
        slots = [nc.alloc_sbuf_tensor("slot%d" % i, [128, T], F32) for i in range(nslot)]
        slots_b = [s.bitcast(BF16) for s in slots]
        spg = [[Page() for _ in range(8)] for _ in range(nslot)]
        slot_dsem_sw = [self.new_dma_sem("slotsw%d" % i) for i in range(nslot)]
        slot_dsem_hw = [self.new_dma_sem("slothw%d" % i) for i in range(nslot)]

        def fv(s, c0=0, c1=T, step=None):
            pgs = spg[s][(c0 * 4) // 1024:((c1 * 4 - 1) // 1024) + 1]
            return TT(slots[s][:, c0:c1], pgs)

        def fvs(s, start, step):
            return TT(slots[s][:, start::step], spg[s])

        def bv(s, c0=0, c1=2 * T):
            pgs = spg[s][(c0 * 2) // 1024:((c1 * 2 - 1) // 1024) + 1]
            return TT(slots_b[s][:, c0:c1], pgs)

        def bchunk(s, half, q=None, lo=None, hi=None):
            base = half * T
            if q is not None:
                return bv(s, base + q * QW, base + (q + 1) * QW)
            if lo is not None:
                return bv(s, base + lo, base + hi)
            return bv(s, base, base + T)

        def slab3(s, K, ncols, k, c0, c1):
            return bv(s, k * ncols + c0, k * ncols + c1)

        PS = nc.alloc_psum_tensor("ps_all", [128, 8, QW], F32)
        bpg = [Page() for _ in range(8)]
        self.pair_rr = 0

        def bank_pair():
            while True:
                p_ = self.pair_rr
                self.pair_rr = (p_ + 1) % 4
                if (2 * p_) not in self.reserved and (2 * p_ + 1) not in self.reserved:
                    return p_

        def bkpair(p_):
            return TT(PS[:, 2 * p_:2 * p_ + 2, :], [bpg[2 * p_], bpg[2 * p_ + 1]])

        self.reserved = set()

        def bank(ncols=QW, c0=0):
            while True:
                i = self.bank_rr
                self.bank_rr = (i + 1) % 8
                if i not in self.reserved:
                    return i

        def bk(i, c0=0, c1=QW):
            return TT(PS[:, i, c0:c1], [bpg[i]])

        DR = lambda ap: TT(ap, [])

        psem = self.new_dma_sem("par")
        sp = self.sp
        pool = self.pool
        self.dma(sp, sm((cvt, cv_pg)), DR(cv), psem)
        self.dma(sp, sm((plt, pl_pg)), DR(pl), psem)
        self.dma(sp, sm((fgt, fg_pg)), DR(fg), psem)
        self.dma(sp, sm((h0t, h0_pg)), DR(h0d), psem)
        self.dma(sp, sm((misct, misc_pg)), DR(misc), psem)
        self.dma(sp, sm((cst, cs_pg)), DR(csd), psem)
        ev = (psem[2], psem[0], psem[1], None)
        for pg_ in (cv_pg, pl_pg, fg_pg, h0_pg, misc_pg, cs_pg):
            pg_.w = ev
        for q in range(NQ):
            xsem = self.new_dma_sem("x%d" % q)
            for c in range(8):
                self.dma(sp, xv(c, q), DR(xT[c * 128:(c + 1) * 128, q * QW:(q + 1) * QW]), xsem)
            ev = (xsem[2], xsem[0], xsem[1], None)
            for c in range(8):
                xpg[c][q].w = ev

        pe_, ae, ve, ge = self.pe, self.act, self.dve, self.pool
        ge.deps([], [sm((identf, ident_pg))])
        ins = ge.e.memset(identf[:], 0.0)
        ge.done(ins, [], [sm((identf, ident_pg))])
        ge.deps([], [sm((identf, ident_pg))])
        ins = ge.e.affine_select(out=identf[:], in_=identf[:], pattern=[[-1, 128]], compare_op=ALU.not_equal,
                                 fill=1.0, base=0, channel_multiplier=1)
        ge.done(ins, [], [sm((identf, ident_pg))])
        ge.deps([], [sm((onesb, ones_pg))])
        ins = ge.e.memset(onesb[:], 1.0 / D)
        ge.done(ins, [], [sm((onesb, ones_pg))])
        for (tt_, pg_, val_) in ((epst, eps_pg, EPS), (onet, one_pg, 1.0), (q25t, q25_pg, 0.25), (nst, ns_pg, 0.0)):
            ge.deps([], [sm((tt_, pg_))])
            ins = ge.e.memset(tt_[:], val_)
            ge.done(ins, [], [sm((tt_, pg_))])

        self.activation(sm((csb, csb_pg)), sm((cvt, cv_pg)), AF.Silu)

        def plv(l, off, n):
            return TT(plt[:, l * NP + off:l * NP + off + n], [pl_pg])

        def derv(l, off, n):
            return TT(der[:, l * 160 + off:l * 160 + off + n], [der_pg])

        D_HBA, D_HBX, D_KKH, D_KK, D_NFW, D_SP = 0, 16, 32, 48, 64, 96
        flag = TT(misct[:, 16:17], [misc_pg])
        for l in range(nl):
            self.activation(derv(l, D_SP, 16), plv(l, P_LAM, 16), AF.Exp, scale=-1.0)
        for l in range(nl):
            self.activation(derv(l, D_SP, 16), derv(l, D_SP, 16), AF.Ln, bias=sm((onet, one_pg)))
        for l in range(nl):
            self.v_ts(derv(l, D_KK, 16), derv(l, D_SP, 16), -8.0, None, ALU.mult)
            self.v_ts(derv(l, D_KKH, 16), derv(l, D_SP, 16), -4.0, None, ALU.mult)
            self.v_ts(derv(l, D_HBA, 16), plv(l, P_BA, 16), 0.5, None, ALU.mult)
            self.v_ts(derv(l, D_HBX, 16), plv(l, P_BX, 16), 0.5, None, ALU.mult)
            self.v_ts(derv(l, D_NFW, 32), plv(l, P_CONVW, 32), flag, None, ALU.mult)
            self.v_ts(derv(l, D_NFW, 32), derv(l, D_NFW, 32), -1.0, None, ALU.mult)

        def load_slab(q, s, parts):
            allp = []
            slot_dsem = slot_dsem_sw if q is self.pool else slot_dsem_hw
            for (c0, K, ncols, dap) in parts:
                pgs = spg[s][(c0 * 2) // 1024:(((c0 + K * ncols) * 2 - 1) // 1024) + 1]
                view = slots_b[s][:, c0:c0 + K * ncols].rearrange("p (k n) -> p k n", k=K)
                self.dma(q, TT(view, pgs), DR(dap), slot_dsem[s])
                allp += pgs
            ds = slot_dsem[s]
            for pg_ in allp:
                pg_.w = (ds[2], ds[0], ds[1], None)

        def load_flat(q, s, parts):
            allp = []
            slot_dsem = slot_dsem_sw if q is self.pool else slot_dsem_hw
            for (c0, ncols, dap) in parts:
                pgs = spg[s][(c0 * 2) // 1024:(((c0 + ncols) * 2 - 1) // 1024) + 1]
                self.dma(q, TT(slots_b[s][:, c0:c0 + ncols], pgs), DR(dap), slot_dsem[s])
                allp += pgs
            ds = slot_dsem[s]
            for pg_ in allp:
                pg_.w = (ds[2], ds[0], ds[1], None)

        def wslab(W2d, k0, K, c0, ncols):
            return W2d[k0 * 128:(k0 + K) * 128, c0:c0 + ncols].rearrange("(k p) n -> p k n", p=128)

        class Ring:
            def __init__(self, ids):
                self.ids = ids
                self.i = 0

            def next(self):
                s = self.ids[self.i % len(self.ids)]
                self.i += 1
                return s

        self.ssq_pending = []
        self.ssq_i = 0

        def ssq_accum(b, m, q):
            sq = sm(scrb[self.ssq_i % 3])
            self.ssq_i += 1
            self.activation(sq, xv(m, q), AF.Square)
            self.ssq_pending.append((b, sq, m))

        def ssq_flush(keep):
            while len(self.ssq_pending) > keep:
                b, sq, m = self.ssq_pending.pop(0)
                self.mm(bk(b), sm((onesb, ones_pg)), sq, m == 0, m == 7, force_inc=True)

        def rmsnorm_to(l_sc, l_sh, hslots, rslot, final=False, osem=None, ssq=None):
            def stats(q):
                if ssq is not None:
                    b = ssq[q]
                else:
                    b = bank()
                    for c in range(8):
                        sq = sm(scrb[c % 2])
                        if c % 2 == 0:
                            self.activation(sq, xv(c, q), AF.Square)
                        else:
                            self.v_tt(sq, xv(c, q), xv(c, q), ALU.mult, eng=self.pool)
                        self.mm(bk(b), sm((onesb, ones_pg)), sq, c == 0, c == 7, force_inc=True)
                r = fv(rslot, q * QW, (q + 1) * QW)
                self.activation(r, bk(b), AF.Sqrt, bias=sm((epst, eps_pg)))
                self.reserved.discard(b)
                self.v_recip(r, r)

            def apply(q):
                for c in range(8):
                    r = fv(rslot, q * QW, (q + 1) * QW)
                    if final:
                        self.v_stt(xv(c, q), xv(c, q), TT(fgt[:, c:c + 1], [fg_pg]), r, ALU.mult, ALU.mult)
                        self.dma(sp, DR(yT[c * 128:(c + 1) * 128, q * QW:(q + 1) * QW]), xv(c, q), osem)
                    else:
                        tmp = sm(scr[self.tmp_i % 4])
                        self.tmp_i += 1
                        self.v_tt(tmp, xv(c, q), r, ALU.mult)
                        self.activation(bchunk(hslots[c // 2], c % 2, q), tmp, AF.Identity,
                                        bias=l_sh(c), scale=l_sc(c))
            stats(0)
            stats(1)
            apply(0)
            stats(2)
            apply(1)
            stats(3)
            apply(2)
            apply(3)

        self.tmp_i = 0
        self.pending_rec = None
        self.ssq_next = None
        self.unit_ctr = 0
        HS = [0, 1, 2, 3]
        RG = [4, 5, 6, 7]
        WS = list(range(8, nslot))

        def hq(k, q):
            return bchunk(HS[k // 2], k % 2, q)

        def proj_chunk(l, col0, q, slab_s, slab_c0, ncols_slab):
            b = bank()
            for k in range(8):
                self.mm(bk(b), slab3(slab_s, 8, ncols_slab, k, slab_c0, slab_c0 + 128), hq(k, q), k == 0, k == 7)
            return b

        def ada_finish(l, bmod, part):
            mod, mod_pg = modt[l]
            sc, sc_pg = sct[l]
            if part in (0, 2):
                self.v_tt(TT(mod[:, 0:16], [mod_pg]), bk(bmod, 0, 16), plv(l, P_ADAB, 16), ALU.add)
                self.v_stt(TT(sc[:, 0:8], [sc_pg]), TT(mod[:, 8:16], [mod_pg]), 1.0, plv(l, P_N1G, 8), ALU.add, ALU.mult)
            if part in (1, 2):
                self.v_tt(TT(mod[:, 16:48], [mod_pg]), bk(bmod, 16, 48), plv(l, P_ADAB + 16, 32), ALU.add)
                self.v_stt(TT(sc[:, 8:16], [sc_pg]), TT(mod[:, 32:40], [mod_pg]), 1.0, plv(l, P_N2G, 8), ALU.add, ALU.mult)

        scr_b = [t.bitcast(BF16) for (t, _) in scr]
        scr_dsem = [self.new_dma_sem("scr%d" % i) for i in range(len(scr))]

        def ada_mini(l, bmod, j0, j1):
            for j in range(j0, j1):
                i = j % len(scr)
                pg = scr[i][1]
                self.dma(pool, TT(scr_b[i][:, 0:1024], [pg]), DR(awd[l * 48 + j]), scr_dsem[i])
                for k in range(8):
                    self.mm(bk(bmod, j, j + 1), TT(scr_b[i][:, k * 128:(k + 1) * 128], [pg]),
                            TT(csb[:, k:k + 1], [csb_pg]), k == 0, k == 7)

        for l in range(nl):
            mod, mod_pg = modt[l]
            sc, sc_pg = sct[l]
            if l == 0:
                ring = Ring(WS[:4])
                bmod = bank()
                self.reserved.add(bmod)
                bmod0 = bmod
                for g in range(4):
                    s = ring.next()
                    load_flat(pool, s, [(jj * 1024, 1024, awd[l * 48 + g * 4 + jj]) for jj in range(4)])
                    for jj in range(4):
                        j = g * 4 + jj
                        for k in range(8):
                            self.mm(bk(bmod, j, j + 1), bv(s, jj * 1024 + k * 128, jj * 1024 + (k + 1) * 128),
                                    TT(csb[:, k:k + 1], [csb_pg]), k == 0, k == 7)
                ada_finish(l, bmod, 0)
            modc = lambda i: TT(mod[:, i:i + 1], [mod_pg])
            rmsnorm_to(lambda c: TT(sc[:, c:c + 1], [sc_pg]), lambda c: modc(c), HS, WS[4], ssq=self.ssq_next)
            self.ssq_next = None

            ring = Ring(WS[:3])
            for g in range(2):
                s = ring.next()
                ncy = QW if g == 0 else 256
                load_slab(pool, s, [(0, 8, ncy, wslab(w_in[l], 0, 8, D_RNN + g * QW, ncy))])
                for q in range(NQ):
                    for jj in range(ncy // 128):
                        c = g * 4 + jj
                        b = proj_chunk(l, 0, q, s, jj * 128, ncy)
                        self.activation(bchunk(RG[c // 2], c % 2, q), bk(b), AF.Gelu_apprx_tanh,
                                        bias=plv(l, P_BIN + 8 + c, 1))

            W_HB, W_A, W_OM, W_IB, W_HF, W_SL = WS[0], WS[2], WS[3], WS[4], WS[5], WS[6]
            if l + 1 < nl:
                bmod_next = bank()
                self.reserved.add(bmod_next)
            if not hasattr(self, "sl_sems"):
                self.sl_sems = [self.new_dma_sem("slx"), self.new_dma_sem("slg")]

            def load_sl_part(parts, dsem):
                allp = []
                for (c0, K, ncols, dap) in parts:
                    pgs = spg[W_SL][(c0 * 2) // 1024:(((c0 + K * ncols) * 2 - 1) // 1024) + 1]
                    view = slots_b[W_SL][:, c0:c0 + K * ncols].rearrange("p (k n) -> p k n", k=K)
                    self.dma(pool, TT(view, pgs), DR(dap), dsem)
                    allp += pgs
                for pg_ in allp:
                    pg_.w = (dsem[2], dsem[0], dsem[1], None)

            def load_sl_x(l_, hd_):
                load_sl_part([(0, 8, 256, wslab(w_in[l_], 0, 8, hd_ * 256, 256))], self.sl_sems[0])

            def load_sl_g(l_, hd_):
                load_sl_part([(2048, 2, 256, wslab(lru_wa[l_, 0, hd_], 0, 2, 0, 256)),
                              (2560, 2, 256, wslab(lru_wx[l_, 0, hd_], 0, 2, 0, 256)),
                              (3072, 2, 256, wslab(lru_wa[l_, 1, hd_], 0, 2, 0, 256)),
                              (3584, 2, 256, wslab(lru_wx[l_, 1, hd_], 0, 2, 0, 256))], self.sl_sems[1])

            XC = [RG[3], WS[1]]

            def prologue_a(hd):
                W_RAW = XC[hd % 2]
                for cc in range(2):
                    c = hd * 2 + cc
                    for q in range(NQ):
                        b = proj_chunk(l, 0, q, W_SL, cc * 128, 256)
                        self.activation(bchunk(W_RAW, cc, q), bk(b), AF.Identity, bias=plv(l, P_BIN + c, 1))
                if hd < 3:
                    load_sl_x(l, hd + 1)
                if hd == 3:
                    load_sl_part([(0, 8, 256, wslab(w_in[l], 0, 8, D_RNN + 768, 256))], self.sl_sems[0])

            def prologue_b(hd):
                W_RAW = XC[hd % 2]
                W_CONV = W_RAW
                for cc in range(2):
                    c = hd * 2 + cc
                    dg, dg_pg = diag[0]
                    for k in range(4):
                        self.v_ts(TT(dg[:, k, :], [dg_pg]), sm((identf, ident_pg)), plv(l, P_CONVW + k * 8 + c, 1), None, ALU.mult)
                    for i, k in enumerate((0, 1, 3)):
                        self.v_ts(TT(dg[:, 4 + i, :], [dg_pg]), sm((identf, ident_pg)), derv(l, D_NFW + k * 8 + c, 1), None, ALU.mult)
                    raw_all = bchunk(W_RAW, cc)
                    rawb = slots_b[W_RAW]
                    base = cc * T

                    def rawv(lo, hi, step=None):
                        if step is None:
                            return TT(rawb[:, base + lo:base + hi], raw_all.pages)
                        return TT(rawb[:, base + lo:base + hi:step], raw_all.pages)

                    cbanks = []
                    for q in range(NQ):
                        b = bank()
                        cbanks.append(b)
                        t0 = q * QW
                        self.mm(bk(b), TT(dg[:, 2, :], [dg_pg]), rawv(t0, t0 + QW), True, False)
                        for k in (0, 1, 3):
                            off = k - 2
                            lo = max(t0, -off)
                            hi = min(t0 + QW, T - off) if off > 0 else t0 + QW
                            hi = min(hi, T)
                            self.mm(bk(b, lo - t0, hi - t0), TT(dg[:, k, :], [dg_pg]), rawv(lo + off, hi + off), False, False)
                        starts = [s0 for s0 in (t0, t0 + SEGW) if s0 > 0]
                        ends = [e for e in (t0 + SEGW - 1, t0 + QW - 1) if e < T - 1]
                        fix = []
                        for s0 in starts:
                            fix.append((s0, 4, s0 - 2))
                            fix.append((s0, 5, s0 - 1))
                            fix.append((s0 + 1, 4, s0 - 1))
                        for e in ends:
                            fix.append((e, 6, e + 1))
                        for i, (oc, di, ic) in enumerate(fix):
                            self.mm(bk(b, oc - t0, oc - t0 + 1), TT(dg[:, di, :], [dg_pg]), rawv(ic, ic + 1), False, i == len(fix) - 1)
                    for q in range(NQ):
                        self.activation(bchunk(W_CONV, cc, q), bk(cbanks[q]), AF.Identity, bias=plv(l, P_CONVB + c, 1))

            def jit_gelu_head3():
                for q in range(NQ):
                    for jj in range(2):
                        c = 6 + jj
                        b = proj_chunk(l, 0, q, W_SL, jj * 128, 256)
                        self.activation(bchunk(RG[3], jj, q), bk(b), AF.Gelu_apprx_tanh,
                                        bias=plv(l, P_BIN + 8 + c, 1))

            def flush_rec():
                if self.pending_rec is None:
                    return
                c = self.pending_rec
                self.pending_rec = None
                for hh_ in range(2):
                    lo, hi = hh_ * 1024, (hh_ + 1) * 1024
                    self.v_tt(fv(W_HF, lo, hi), fv(W_HF, lo, hi), fv(W_HB, lo, hi), ALU.add)
                    self.v_tt(bchunk(RG[c // 2], c % 2, None, lo, hi), fv(W_HF, lo, hi),
                              bchunk(RG[c // 2], c % 2, None, lo, hi), ALU.mult)

            for hd in range(4):
                ada_jobs = []
                if l == 0:
                    ada_jobs += [(0, bmod0, j) for j in range(16 + hd * 8, 16 + (hd + 1) * 8)]
                if l + 1 < nl:
                    ada_jobs += [(l + 1, bmod_next, j) for j in range(hd * 12, (hd + 1) * 12)]
                if hd == 0:
                    load_sl_x(l, 0)
                    load_sl_g(l, 0)
                    prologue_a(0)
                    prologue_b(0)
                W_CONV = XC[hd % 2]
                tiny4 = [TT(tiny[i][0][:, 0:4], [tiny[i][1]]) for i in range(4)]
                unit_list = []
                for cc in range(2):
                    for d in range(2):
                        for hv in range(2):
                            unit_list.append((cc, d, hv, hv if d == 0 else 1 - hv))
                for pi in range(4):
                    pair = unit_list[2 * pi:2 * pi + 2]
                    binfo = []
                    if pi == 3 and hd < 3:
                        prologue_a(hd + 1)
                    if pi == 1 and hd == 3:
                        jit_gelu_head3()
                    for ui, (cc, d, hv, th) in enumerate(pair):
                        c = hd * 2 + cc
                        pidx = d * 8 + c
                        wa0 = 2048 + d * 1024
                        wx0 = 2560 + d * 1024
                        k = self.unit_ctr % 2
                        self.unit_ctr += 1
                        binfo.append(k)
                        pr, pi_ = bank_pair(), bank_pair()
                        for qq in range(2):
                            q = 2 * th + qq
                            for kc in range(2):
                                self.mm(bk(2 * pr + qq), bv(W_SL, wa0 + kc * 256 + cc * 128, wa0 + kc * 256 + cc * 128 + 128),
                                        bchunk(W_CONV, kc, q), kc == 0, kc == 1)
                            for kc in range(2):
                                self.mm(bk(2 * pi_ + qq), bv(W_SL, wx0 + kc * 256 + cc * 128, wx0 + kc * 256 + cc * 128 + 128),
                                        bchunk(W_CONV, kc, q), kc == 0, kc == 1)
                        Ak = fv(W_A, k * 1024, (k + 1) * 1024)
                        Ik = fv(W_IB, k * 1024, (k + 1) * 1024)
                        A3 = TT(slots[W_A][:, k * 1024:(k + 1) * 1024].rearrange("p (a b) -> p a b", a=2), Ak.pages)
                        I3 = TT(slots[W_IB][:, k * 1024:(k + 1) * 1024].rearrange("p (a b) -> p a b", a=2), Ik.pages)
                        self.activation(A3, bkpair(pr), AF.Tanh, bias=derv(l, D_HBA + pidx, 1), scale=0.5)
                        self.activation(Ak, Ak, AF.Exp, bias=derv(l, D_KKH + pidx, 1), scale=derv(l, D_KKH + pidx, 1))
                        self.activation(I3, bkpair(pi_), AF.Tanh, bias=derv(l, D_HBX + pidx, 1), scale=0.5)
                    if pi == 3 and hd < 3:
                        load_sl_g(l, hd + 1)
                        prologue_b(hd + 1)
                    nj = (len(ada_jobs) + (3 - pi)) // (4 - pi)
                    for (al, ab, aj) in ada_jobs[:nj]:
                        ada_mini(al, ab, aj, aj + 1)
                    ada_jobs = ada_jobs[nj:]
                    for ui, (cc, d, hv, th) in enumerate(pair):
                        k = binfo[ui]
                        Ak = fv(W_A, k * 1024, (k + 1) * 1024)
                        Ok = fv(W_OM, k * 1024, (k + 1) * 1024)
                        self.v_stt(Ok, Ak, 1.0 - 2e-5, Ak, ALU.min, ALU.mult)
                    flush_rec()
                    for ui, (cc, d, hv, th) in enumerate(pair):
                        k = binfo[ui]
                        Ok = fv(W_OM, k * 1024, (k + 1) * 1024)
                        self.activation(Ok, Ok, AF.Sqrt, bias=sm((q25t, q25_pg)), scale=-0.25)
                    for ui, (cc, d, hv, th) in enumerate(pair):
                        c = hd * 2 + cc
                        k = binfo[ui]
                        Ak = fv(W_A, k * 1024, (k + 1) * 1024)
                        Ok = fv(W_OM, k * 1024, (k + 1) * 1024)
                        Ik = fv(W_IB, k * 1024, (k + 1) * 1024)
                        self.v_stt(Ik, Ik, 1.0, bchunk(W_CONV, cc, None, th * 1024, (th + 1) * 1024), ALU.add, ALU.mult)
                        self.v_tt(Ik, Ik, Ok, ALU.mult)
                        st = 0 if d == 0 else SEGW - 1
                        h0v = TT(h0t[:, (l * 2 + d) * 8 + c:(l * 2 + d) * 8 + c + 1], [h0_pg])
                        kpv = TT(misct[:, d * 8 + 4 * th:d * 8 + 4 * th + 4], [misc_pg])
                        selv = TT(misct[:, 17 + d * 8 + 4 * th:17 + d * 8 + 4 * th + 4], [misc_pg])
                        av = TT(slots[W_A][:, k * 1024 + st:(k + 1) * 1024:SEGW], Ak.pages)
                        ibv = TT(slots[W_IB][:, k * 1024 + st:(k + 1) * 1024:SEGW], Ik.pages)
                        tn = tiny4[self.unit_ctr % 4]
                        if hv == 0:
                            self.v_stt(tn, av, h0v, selv, ALU.mult, ALU.mult)
                            self.v_tt(ibv, ibv, tn, ALU.add)
                        self.v_tt(av, av, kpv, ALU.mult)
                        if d == 0:
                            outv = fv(W_HF, th * 1024, (th + 1) * 1024)
                            ini = None if hv == 0 else TT(slots[W_HF][:, th * 1024 - 1:th * 1024], fv(W_HF, th * 1024 - 1, th * 1024).pages)
                            self.v_scan(outv, Ak, Ik, ini)
                        else:
                            pg_o = fv(W_HB, th * 1024, (th + 1) * 1024).pages
                            outv = TT(slots[W_HB][:, th * 1024:(th + 1) * 1024][:, ::-1], pg_o)
                            ini = None if hv == 0 else TT(slots[W_HB][:, (th + 1) * 1024:(th + 1) * 1024 + 1],
                                                          fv(W_HB, (th + 1) * 1024, (th + 1) * 1024 + 1).pages)
                            self.v_scan(outv, TT(slots[W_A][:, k * 1024:(k + 1) * 1024][:, ::-1], Ak.pages),
                                        TT(slots[W_IB][:, k * 1024:(k + 1) * 1024][:, ::-1], Ik.pages), ini)
                        if hv == 1:
                            hs = W_HF if d == 0 else W_HB
                            fin = SEGW - 1 if d == 0 else 0
                            o = ((l * 2 + d) * 8 + c) * NSEG
                            self.v_copy(TT(nst[:, o:o + NSEG], [ns_pg]), fvs(hs, fin, SEGW))
                    if pi % 2 == 1:
                        self.pending_rec = hd * 2 + pair[0][0]
                if hd == 3:
                    flush_rec()
                    if l == 0:
                        ada_finish(0, bmod0, 1)
                        self.reserved.discard(bmod0)
                    if l + 1 < nl:
                        ada_finish(l + 1, bmod_next, 2)
                        self.reserved.discard(bmod_next)

            FO = [WS[0], WS[1]]
            XF, Y0, Y1 = WS[2], WS[3], WS[4]
            tring = Ring(WS[5:])
            for hh in range(2):
                load_slab(pool, tring.ids[0], [(0, 8, 256, wslab(w_in[l], 0, 8, 2 * D_RNN + hh * 256, 256))])
                ws = tring.ids[0]
                tring.i = 1
                for fl in range(2):
                    fc = hh * 2 + fl
                    for q in range(NQ):
                        b = proj_chunk(l, 0, q, ws, fl * 128, 256)
                        self.activation(bchunk(XF, fl, q), bk(b), AF.Identity, bias=plv(l, P_BIN + 16 + fc, 1))
                for tt in range(16):
                    b = bank()
                    for fl in range(2):
                        self.mm(bk(b, fl * 256, (fl + 1) * 256), bchunk(XF, fl, None, tt * 128, (tt + 1) * 128),
                                sm((cst, cs_pg)), True, True)
                    ys = Y0 if tt < 8 else Y1
                    self.v_copy(bv(ys, (tt % 8) * QW, (tt % 8 + 1) * QW), bk(b))
                units = [(sl, hf_) for sl in (tring.ids + [XF]) for hf_ in range(2)]
                if hh == 0:
                    self.unit_i = 2
                if not hasattr(self, "unit_sems"):
                    self.unit_sems = {}
                for u_ in units:
                    if u_ not in self.unit_sems:
                        self.unit_sems[u_] = self.new_dma_sem("unit%d_%d" % u_)
                for pt in range(NQ):
                    bb = [bank(), bank()]
                    for g8 in range(8):
                        sl, hf_ = units[self.unit_i % len(units)]
                        self.unit_i += 1
                        ub = hf_ * 2048
                        uv_ = bv(sl, ub, ub + 2048)
                        self.dma(sp, uv_, DR(tabd[pt * 8 + g8]), self.unit_sems[(sl, hf_)])
                        for fl in range(2):
                            for tl in range(2):
                                tt = g8 * 2 + tl
                                ys = Y0 if tt < 8 else Y1
                                yb = (tt % 8) * QW + fl * 256
                                self.mm(bk(bb[fl]), bv(ys, yb, yb + 128), bv(sl, ub + tl * 1024, ub + tl * 1024 + QW), tt == 0, False)
                                self.mm(bk(bb[fl]), bv(ys, yb + 128, yb + 256), bv(sl, ub + tl * 1024 + QW, ub + tl * 1024 + 2 * QW), False, tt == 15,
                                        force_inc=(tl == 1))
                    for fl in range(2):
                        self.v_copy(bchunk(FO[hh], fl, pt), bk(bb[fl]))

            MGH = [WS[2], WS[3]]
            ring = Ring(WS[4:7])
            ssq2 = [None] * NQ

            def mgv(m, qq):
                col = m * 1024 + qq * QW
                return bv(MGH[col // 4096], col % 4096, col % 4096 + QW)

            for th in range(2):
                for qq in range(2):
                    b_ = bank()
                    self.reserved.add(b_)
                    ssq2[2 * th + qq] = b_
                for m in range(8):
                    s = ring.next()
                    load_flat(pool, s, [(0, 3584, mwd[l * 8 + m])])
                    for qq in range(2):
                        q = 2 * th + qq
                        ba_ = bank()
                        for k in range(8):
                            self.mm(bk(ba_), bv(s, k * 128, (k + 1) * 128), bchunk(RG[k // 2], k % 2, q), k == 0, k == 7)
                        bb_ = bank()
                        for k in range(4):
                            self.mm(bk(bb_), bv(s, 1024 + k * 128, 1024 + (k + 1) * 128), bchunk(FO[k // 2], k % 2, q), k == 0, k == 3)
                        bga = bank()
                        for k in range(8):
                            self.mm(bk(bga), bv(s, 1536 + k * 128, 1536 + (k + 1) * 128), hq(k, q), k == 0, k == 7)
                        bgb = bank()
                        for k in range(8):
                            self.mm(bk(bgb), bv(s, 2560 + k * 128, 2560 + (k + 1) * 128), hq(k, q), k == 0, k == 7)
                        ga = sm(scr[(qq % 2) * 2])
                        gb = sm(scr[1 + (qq % 2) * 2])
                        self.activation(ga, bk(bga), AF.Sigmoid, bias=plv(l, P_BIN + 20 + m, 1))
                        self.activation(gb, bk(bgb), AF.Sigmoid, bias=plv(l, P_BIN + 28 + m, 1))
                        self.v_tt(ga, ga, bk(ba_), ALU.mult)
                        self.v_tt(gb, gb, bk(bb_), ALU.mult)
                        self.v_tt(mgv(m, qq), ga, gb, ALU.add)
                for g in range(2):
                    s = ring.next()
                    load_slab(pool, s, [(0, 8, QW, wslab(w_out[l], 0, 8, g * QW, QW))])
                    for jj in range(4):
                        m = g * 4 + jj
                        for qq in range(2):
                            q = 2 * th + qq
                            b = bank()
                            for k in range(8):
                                self.mm(bk(b), slab3(s, 8, QW, k, jj * 128, (jj + 1) * 128), mgv(k, qq), k == 0, k == 7)
                            ssq_flush(2)
                            self.v_stt(xv(m, q), bk(b), modc(16 + m), xv(m, q), ALU.mult, ALU.add)
                            ssq_accum(ssq2[q], m, q)
                ssq_flush(0)

            rmsnorm_to(lambda c: TT(sc[:, 8 + c:9 + c], [sc_pg]), lambda c: modc(24 + c), HS, WS[0], ssq=ssq2)

            ACTS = RG + WS[0:2]
            ring = Ring(WS[2:])
            for hf in range(2):
                for g in range(6):
                    ncg = QW if g < 5 else 256
                    su = ring.next()
                    load_slab(pool, su, [(0, 8, ncg, wslab(ffn_w_in[l], 0, 8, g * QW, ncg))])
                    sv_ = ring.next()
                    load_slab(pool, sv_, [(0, 8, ncg, wslab(ffn_w_in[l], 0, 8, D_FF + g * QW, ncg))])
                    for qq in range(2):
                        q = hf * 2 + qq
                        for jj in range(ncg // 128):
                            j = g * 4 + jj
                            bu = bank()
                            for k in range(8):
                                self.mm(bk(bu), slab3(su, 8, ncg, k, jj * 128, (jj + 1) * 128), hq(k, q), k == 0, k == 7)
                            bv_ = bank()
                            for k in range(8):
                                self.mm(bk(bv_), slab3(sv_, 8, ncg, k, jj * 128, (jj + 1) * 128), hq(k, q), k == 0, k == 7)
                            su_t = sm(scr[2 + (jj % 2)])
                            self.activation(su_t, bk(bu), AF.Silu)
                            col = j * 1024 + qq * QW
                            self.v_tt(bv(ACTS[col // 4096], col % 4096, col % 4096 + QW), su_t, bk(bv_), ALU.mult)
                if hf == 0:
                    self.ssq_next = [None] * NQ
                for qq in range(2):
                    b_ = bank()
                    self.reserved.add(b_)
                    self.ssq_next[hf * 2 + qq] = b_
                for mp in range(4):
                    s0 = ring.next()
                    load_slab(pool, s0, [(0, 11, 256, wslab(ffn_w_out[l], 0, 11, mp * 256, 256))])
                    s1 = ring.next()
                    load_slab(pool, s1, [(0, 11, 256, wslab(ffn_w_out[l], 11, 11, mp * 256, 256))])
                    for mm_ in range(2):
                        m = mp * 2 + mm_
                        for qq in range(2):
                            q = hf * 2 + qq
                            b = bank()
                            for j in range(NJ):
                                ss = s0 if j < 11 else s1
                                col = j * 1024 + qq * QW
                                self.mm(bk(b), slab3(ss, 11, 256, j % 11, mm_ * 128, (mm_ + 1) * 128),
                                        bv(ACTS[col // 4096], col % 4096, col % 4096 + QW), j == 0, j == NJ - 1)
                            ssq_flush(1)
                            self.v_stt(xv(m, q), bk(b), modc(40 + m), xv(m, q), ALU.mult, ALU.add)
                            ssq_accum(self.ssq_next[q], m, q)
                ssq_flush(0)

        osem = self.new_dma_sem("out")
        rmsnorm_to(None, None, None, WS[0], final=True, osem=osem, ssq=self.ssq_next)
        self.dma(sp, DR(nsd), sm((nst, ns_pg)), osem)
        sp.e.wait_ge(osem[0], osem[1])
        return nc


_CACHE = {}


def _dft_tables():
    if "t" in _CACHE:
        return _CACHE["t"]
    bf = ml_dtypes.bfloat16
    ch = np.arange(64)
    th = 2 * np.pi * np.outer(ch, ch) / 64.0
    cs = np.zeros((128, 256), np.float64)
    for g in range(2):
        cs[g * 64:(g + 1) * 64, g * 64:(g + 1) * 64] = np.cos(th) / 8.0
        cs[g * 64:(g + 1) * 64, 128 + g * 64:128 + (g + 1) * 64] = np.sin(th) / 8.0
    p = np.arange(T)
    r, c = p // 64, p % 64
    ths = 2 * np.pi * (np.outer(r, r) / 32.0 + np.outer(c, c) / 64.0)
    cp_s = (np.cos(ths) / np.sqrt(T)).astype(np.float32).astype(bf)
    nsp_s = (-np.sin(ths) / np.sqrt(T)).astype(np.float32).astype(bf)
    q = np.arange(SEGW)
    thp = 2 * np.pi * np.outer(q, q) / float(SEGW)
    cp_p = np.zeros((T, T), np.float32)
    nsp_p = np.zeros((T, T), np.float32)
    for s in range(NSEG):
        cp_p[s * SEGW:(s + 1) * SEGW, s * SEGW:(s + 1) * SEGW] = np.cos(thp) / 16.0
        nsp_p[s * SEGW:(s + 1) * SEGW, s * SEGW:(s + 1) * SEGW] = -np.sin(thp) / 16.0
    def units(cp, nsp):
        a = np.stack([np.asarray(cp), np.asarray(nsp)], 0).reshape(2, 8, 2, 128, 4, 512)
        return np.ascontiguousarray(np.transpose(a, (4, 1, 3, 2, 0, 5))).reshape(32, 128, 2048)
    _CACHE["t"] = (cs.astype(np.float32).astype(bf), units(cp_s, nsp_s), units(cp_p.astype(bf), nsp_p.astype(bf)))
    return _CACHE["t"]


def _pc(v):
    v = np.asarray(v, np.float32)
    sh = v.shape[:-1]
    return np.moveaxis(v.reshape(sh + (8, 128)), -1, 0)


def kernel(x_prompt, x_sample, state_lru, c, c_ctx, norm1_g, norm2_g, ada_w, ada_b, w_in, b_in, conv_w, conv_b,
           lru_wa, lru_ba, lru_wx, lru_bx, lru_lambda, w_lru_out, w_fnet_out, w_out, ffn_w_in, ffn_w_out, final_g):
    n_layers = int(os.environ.get("MK_LAYERS", L))
    f32 = lambda a: np.ascontiguousarray(np.asarray(a, np.float32))
    cs, tab_s, tab_p = _dft_tables()
    plp = np.zeros((128, L, NP), np.float32)
    plp[:, :, P_N1G:P_N1G + 8] = _pc(norm1_g)
    plp[:, :, P_N2G:P_N2G + 8] = _pc(norm2_g)
    plp[:, :, P_ADAB:P_ADAB + 48] = np.moveaxis(f32(ada_b).reshape(L, 48, 128), -1, 0)
    plp[:, :, P_BIN:P_BIN + 36] = np.moveaxis(f32(b_in).reshape(L, 36, 128), -1, 0)
    plp[:, :, P_CONVW:P_CONVW + 32] = _pc(conv_w).reshape(128, L, 32)
    plp[:, :, P_CONVB:P_CONVB + 8] = _pc(conv_b)
    plp[:, :, P_BA:P_BA + 16] = _pc(lru_ba).reshape(128, L, 16)
    plp[:, :, P_BX:P_BX + 16] = _pc(lru_bx).reshape(128, L, 16)
    plp[:, :, P_LAM:P_LAM + 16] = _pc(lru_lambda).reshape(128, L, 16)
    plp = np.ascontiguousarray(plp.reshape(128, L * NP))
    fgp = np.ascontiguousarray(_pc(final_g))
    ada_w = f32(ada_w)
    w_in = f32(w_in)
    aw = np.ascontiguousarray(np.transpose(ada_w.reshape(L, 8, 128, 48, 128), (0, 3, 2, 1, 4))).reshape(L * 48, 128, 1024)
    def blk(W, c0):
        K = W.shape[1] // 128
        return np.transpose(W[:, :, c0:c0 + 1024].reshape(L, K, 128, 8, 128), (0, 3, 2, 1, 4))
    mw = np.concatenate([blk(f32(w_lru_out), 0), blk(f32(w_fnet_out), 0), blk(w_in, 2 * D_RNN + D_FNET),
                         blk(w_in, 2 * D_RNN + D_FNET + D)], axis=3)
    mw = np.ascontiguousarray(mw).reshape(L * 8, 128, 3584)
    shared = {"pl": plp, "fg": fgp, "aw": aw, "mw": mw, "w_in": w_in, "lru_wa": f32(lru_wa), "lru_wx": f32(lru_wx),
              "w_out": f32(w_out),
              "ffn_w_in": f32(ffn_w_in), "ffn_w_out": f32(ffn_w_out), "cs": cs}
    x_prompt = f32(x_prompt)
    x_sample = f32(x_sample)
    state_lru = f32(state_lru)
    in_maps = []
    for core in range(8):
        m = dict(shared)
        misc = np.zeros((128, 33), np.float32)
        h0 = np.zeros((128, L, 2, 8), np.float32)
        if core < 4:
            xin = x_sample[core]
            cvec = f32(c)[core]
            m["tab"] = tab_s
            misc[:, 0:8] = 1.0
            misc[:, 0] = 0.0
            misc[:, 8:16] = 1.0
            misc[:, 15] = 0.0
            h0 = _pc(state_lru[core])
            misc[:, 17] = 1.0
            misc[:, 17 + 8 + NSEG - 1] = 1.0
        else:
            pi = (core - 4) % 2
            xin = x_prompt[pi * 8:(pi + 1) * 8].reshape(T, D)
            cvec = f32(c_ctx)
            m["tab"] = tab_p
            misc[:, 16] = 1.0
        m["xT"] = np.ascontiguousarray(xin.T)
        m["cv"] = np.ascontiguousarray(cvec.reshape(8, 128).T)
        m["h0"] = np.ascontiguousarray(h0.reshape(128, -1))
        m["misc"] = misc
        in_maps.append(m)
    key = ("nc", n_layers)
    if key not in _CACHE:
        _CACHE[key] = Builder(n_layers).build()
    nc = _CACHE[key]
    res = run_bass_kernel_spmd(nc, in_maps, core_ids=list(range(8)))
    outs = res.results
    y_sample = np.stack([np.ascontiguousarray(outs[b]["yT"].T) for b in range(4)], 0)
    y_prompt = np.concatenate([np.ascontiguousarray(outs[4 + i]["yT"].T).reshape(8, SEGW, D) for i in range(2)], 0)
    ns = np.zeros((16, L, 2, D_RNN), np.float32)
    for i in range(2):
        a = outs[4 + i]["ns"].reshape(128, L, 2, 8, NSEG)
        ns[i * 8:(i + 1) * 8] = np.transpose(a, (4, 1, 2, 3, 0)).reshape(NSEG, L, 2, D_RNN)
    return (y_prompt.astype(np.float32), y_sample.astype(np.float32), ns)
```

```python
import os
import numpy as np
import ml_dtypes
import concourse.bass as bass
import concourse.mybir as mybir
from concourse.bass_utils import run_bass_kernel_spmd

F32 = mybir.dt.float32
BF16 = mybir.dt.bfloat16
AF = mybir.ActivationFunctionType
ALU = mybir.AluOpType

D = 1024
L = 4
T = 2048
NQ = 4
QW = 512
D_RNN = 1024
D_FNET = 512
N_IN = 4608
D_FF = 2816
NJ = 22
EPS = 1e-6
NSEG = 8
SEGW = 256

P_N1G, P_N2G, P_ADAB, P_BIN, P_CONVW, P_CONVB, P_BA, P_BX, P_LAM = 0, 8, 16, 64, 100, 132, 140, 156, 172
NP = 188


class Page:
    __slots__ = ("w", "rs")

    def __init__(self):
        self.w = None
        self.rs = []


class TT:
    __slots__ = ("ap", "pages")

    def __init__(self, ap, pages):
        self.ap = ap
        self.pages = pages


class Eng:
    def __init__(self, nc, e, name, is_pe=False):
        self.e = e
        self.name = name
        self.sem = nc.alloc_semaphore("sem_" + name)
        self.n = 0
        self.waited = {}
        self.is_pe = is_pe

    def wait(self, ev):
        if ev is None:
            return
        key, h, val, src = ev
        if self.is_pe and src is self:
            return
        if self.waited.get(key, 0) >= val:
            return
        self.e.wait_ge(h, val)
        self.waited[key] = val

    def deps(self, reads, writes):
        for t in reads:
            for p in t.pages:
                self.wait(p.w)
        for t in writes:
            for p in t.pages:
                self.wait(p.w)
                for r in p.rs:
                    self.wait(r)

    def done(self, ins, reads, writes, inc=True):
        if inc:
            self.n += 1
            ins.then_inc(self.sem, 1)
            ev = (self.name, self.sem, self.n, self)
        else:
            ev = (self.name, self.sem, self.n + 1, self)
        for t in reads:
            for p in t.pages:
                p.rs.append(ev)
        for t in writes:
            for p in t.pages:
                p.w = ev
                p.rs = []
        return ev


class Builder:
    def __init__(self, n_layers=L):
        self.n_layers = n_layers
        nc = bass.Bass("TRN2", target_bir_lowering=False)
        self.nc = nc
        self.pe = Eng(nc, nc.tensor, "pe", is_pe=True)
        self.act = Eng(nc, nc.scalar, "act")
        self.dve = Eng(nc, nc.vector, "dve")
        self.pool = Eng(nc, nc.gpsimd, "pool")
        self.sp = Eng(nc, nc.sync, "sp")
        self.nsem = 0
        self.bank_rr = 0

    def dram_in(self, name, shape, dt=F32):
        return self.nc.dram_tensor(name, list(shape), dt, kind="ExternalInput").ap()

    def dram_out(self, name, shape, dt=F32):
        return self.nc.dram_tensor(name, list(shape), dt, kind="ExternalOutput").ap()

    def small(self, name, shape, dt=F32):
        t = self.nc.alloc_sbuf_tensor(name, list(shape), dt)
        pg = Page()
        return t, pg

    def new_dma_sem(self, name):
        self.nsem += 1
        return [self.nc.alloc_semaphore("dsem_%s_%d" % (name, self.nsem)), 0, "d%s%d" % (name, self.nsem)]

    def dma(self, q, out_t, in_t, dsem, out_is_sbuf=True):
        q.deps([in_t], [out_t])
        ins = q.e.dma_start(out=out_t.ap, in_=in_t.ap)
        dsem[1] += 16
        ins.then_inc(dsem[0], 16)
        ev = (dsem[2], dsem[0], dsem[1], None)
        for p in in_t.pages:
            p.rs.append(ev)
        for p in out_t.pages:
            p.w = ev
            p.rs = []
        return ev

    def mm(self, ps, lhsT, rhs, start, stop, force_inc=False):
        pe = self.pe
        pe.deps([lhsT, rhs], [ps] if start else [])
        ins = pe.e.matmul(ps.ap, lhsT=lhsT.ap, rhs=rhs.ap, start=start, stop=stop)
        if stop:
            pe.done(ins, [lhsT, rhs], [ps], inc=True)
        elif force_inc:
            pe.done(ins, [lhsT, rhs], [], inc=True)
        elif False:
            pe.done(ins, [lhsT, rhs], [ps], inc=True)
        else:
            pe.done(ins, [lhsT, rhs], [], inc=False)

    def activation(self, out, in_, func, bias=None, scale=None, extra=()):
        e = self.act
        reads = [in_] + [x for x in (bias, scale) if isinstance(x, TT)] + list(extra)
        e.deps(reads, [out])
        kw = {}
        if bias is not None:
            kw["bias"] = bias.ap if isinstance(bias, TT) else bias
        if scale is not None:
            kw["scale"] = scale.ap if isinstance(scale, TT) else scale
        ins = e.e.activation(out=out.ap, in_=in_.ap, func=func, **kw)
        e.done(ins, reads, [out])

    def v_tt(self, out, in0, in1, op, eng=None):
        e = eng or self.dve
        e.deps([in0, in1], [out])
        ins = e.e.tensor_tensor(out=out.ap, in0=in0.ap, in1=in1.ap, op=op)
        e.done(ins, [in0, in1], [out])

    def v_stt(self, out, in0, scalar, in1, op0, op1):
        e = self.dve
        reads = [in0, in1] + ([scalar] if isinstance(scalar, TT) else [])
        e.deps(reads, [out])
        ins = e.e.scalar_tensor_tensor(out=out.ap, in0=in0.ap, scalar=scalar.ap if isinstance(scalar, TT) else scalar,
                                       in1=in1.ap, op0=op0, op1=op1)
        e.done(ins, reads, [out])

    def v_ts(self, out, in0, s1, s2, op0, op1=None, eng=None):
        e = eng or self.dve
        reads = [in0] + [x for x in (s1, s2) if isinstance(x, TT)]
        e.deps(reads, [out])
        a1 = s1.ap if isinstance(s1, TT) else s1
        a2 = s2.ap if isinstance(s2, TT) else s2
        if op1 is None:
            ins = e.e.tensor_scalar(out=out.ap, in0=in0.ap, scalar1=a1, scalar2=None, op0=op0)
        else:
            ins = e.e.tensor_scalar(out=out.ap, in0=in0.ap, scalar1=a1, scalar2=a2, op0=op0, op1=op1)
        e.done(ins, reads, [out])

    def v_copy(self, out, in_, eng=None):
        e = eng or self.dve
        e.deps([in_], [out])
        ins = e.e.tensor_copy(out=out.ap, in_=in_.ap)
        e.done(ins, [in_], [out])

    def v_recip(self, out, in_):
        e = self.dve
        e.deps([in_], [out])
        ins = e.e.reciprocal(out=out.ap, in_=in_.ap)
        e.done(ins, [in_], [out])

    def v_scan(self, out, d0, d1, initial=None):
        e = self.dve
        reads = [d0, d1] + ([initial] if initial is not None else [])
        e.deps(reads, [out])
        ins = e.e.tensor_tensor_scan(out=out.ap, data0=d0.ap, data1=d1.ap,
                                     initial=(initial.ap if initial is not None else 0.0), op0=ALU.mult, op1=ALU.add)
        e.done(ins, reads, [out])

    def build(self):
        nc = self.nc
        nl = self.n_layers
        xT = self.dram_in("xT", [D, T])
        cv = self.dram_in("cv", [128, 8])
        pl = self.dram_in("pl", [128, L * NP])
        fg = self.dram_in("fg", [128, 8])
        h0d = self.dram_in("h0", [128, L * 2 * 8])
        misc = self.dram_in("misc", [128, 33])
        awd = self.dram_in("aw", [L * 48, 128, 1024])
        mwd = self.dram_in("mw", [L * 8, 128, 3584])
        w_in = self.dram_in("w_in", [L, D, N_IN])
        lru_wa = self.dram_in("lru_wa", [L, 2, 4, 256, 256])
        lru_wx = self.dram_in("lru_wx", [L, 2, 4, 256, 256])
        w_out = self.dram_in("w_out", [L, D, D])
        ffn_w_in = self.dram_in("ffn_w_in", [L, D, 2 * D_FF])
        ffn_w_out = self.dram_in("ffn_w_out", [L, D_FF, D])
        csd = self.dram_in("cs", [128, 256], BF16)
        tabd = self.dram_in("tab", [32, 128, 2048], BF16)
        yT = self.dram_out("yT", [D, T])
        nsd = self.dram_out("ns", [128, L * 2 * 8 * NSEG])

        X = nc.alloc_sbuf_tensor("X", [128, 8, T], F32)
        xpg = [[Page() for _ in range(NQ)] for _ in range(8)]

        def xv(c, q):
            return TT(X[:, c, q * QW:(q + 1) * QW], [xpg[c][q]])

        cvt, cv_pg = self.small("cvt", [128, 8])
        csb, csb_pg = self.small("csb", [128, 8], BF16)
        plt, pl_pg = self.small("plt", [128, L * NP])
        fgt, fg_pg = self.small("fgt", [128, 8])
        h0t, h0_pg = self.small("h0t", [128, L * 2 * 8])
        misct, misc_pg = self.small("misct", [128, 33])
        der, der_pg = self.small("der", [128, L * 160])
        modt = [self.small("mod%d" % l, [128, 48]) for l in range(L)]
        sct = [self.small("sc%d" % l, [128, 16]) for l in range(L)]
        identf, ident_pg = self.small("identf", [128, 128])
        onesb, ones_pg = self.small("onesb", [128, 128], BF16)
        cst, cs_pg = self.small("cst", [128, 256], BF16)
        nst, ns_pg = self.small("nst", [128, L * 2 * 8 * NSEG])
        epst, eps_pg = self.small("epst", [128, 1])
        onet, one_pg = self.small("onet", [128, 1])
        q25t, q25_pg = self.small("q25t", [128, 1])
        diag = [self.small("diag%d" % i, [128, 7, 128], BF16) for i in range(1)]
        scr = [self.small("scr%d" % i, [128, QW]) for i in range(4)]
        scrb = [self.small("scrb%d" % i, [128, QW], BF16) for i in range(3)]
        tiny = [self.small("tiny%d" % i, [128, 8]) for i in range(4)]

        def sm(tp, lo=None, hi=None):
            t, pg = tp
            if lo is None:
                return TT(t[:], [pg])
            return TT(t[:, lo:hi], [pg])

        rem = nc.sbuf_bytes_remaining
        nslot = min(int(rem // 8192), 16)
        print('sbuf rem', rem, 'nslot', nslot)
        assert nslot >= 15, nslot
        slots = [nc.alloc_sbuf_tensor("slot%d" % i, [128, T], F32) for i in range(nslot)]
        slots_b = [s.bitcast(BF16) for s in slots]
        spg = [[Page() for _ in range(8)] for _ in range(nslot)]
        slot_dsem_sw = [self.new_dma_sem("slotsw%d" % i) for i in range(nslot)]
        slot_dsem_hw = [self.new_dma_sem("slothw%d" % i) for i in range(nslot)]

        def fv(s, c0=0, c1=T, step=None):
            pgs = spg[s][(c0 * 4) // 1024:((c1 * 4 - 1) // 1024) + 1]
            return TT(slots[s][:, c0:c1], pgs)

        def fvs(s, start, step):
            return TT(slots[s][:, start::step], spg[s])

        def bv(s, c0=0, c1=2 * T):
            pgs = spg[s][(c0 * 2) // 1024:((c1 * 2 - 1) // 1024) + 1]
            return TT(slots_b[s][:, c0:c1], pgs)

        def bchunk(s, half, q=None, lo=None, hi=None):
            base = half * T
            if q is not None:
                return bv(s, base + q * QW, base + (q + 1) * QW)
            if lo is not None:
                return bv(s, base + lo, base + hi)
            return bv(s, base, base + T)

        def slab3(s, K, ncols, k, c0, c1):
            return bv(s, k * ncols + c0, k * ncols + c1)

        PS = nc.alloc_psum_tensor("ps_all", [128, 8, QW], F32)
        bpg = [Page() for _ in range(8)]
        self.pair_rr = 0

        def bank_pair():
            while True:
                p_ = self.pair_rr
                self.pair_rr = (p_ + 1) % 4
                if (2 * p_) not in self.reserved and (2 * p_ + 1) not in self.reserved:
                    return p_

        def bkpair(p_):
            return TT(PS[:, 2 * p_:2 * p_ + 2, :], [bpg[2 * p_], bpg[2 * p_ + 1]])

        self.reserved = set()

        def bank(ncols=QW, c0=0):
            while True:
                i = self.bank_rr
                self.bank_rr = (i + 1) % 8
                if i not in self.reserved:
                    return i

        def bk(i, c0=0, c1=QW):
            return TT(PS[:, i, c0:c1], [bpg[i]])

        DR = lambda ap: TT(ap, [])

        psem = self.new_dma_sem("par")
        sp = self.sp
        pool = self.pool
        self.dma(sp, sm((cvt, cv_pg)), DR(cv), psem)
        self.dma(sp, sm((plt, pl_pg)), DR(pl), psem)
        self.dma(sp, sm((fgt, fg_pg)), DR(fg), psem)
        self.dma(sp, sm((h0t, h0_pg)), DR(h0d), psem)
        self.dma(sp, sm((misct, misc_pg)), DR(misc), psem)
        self.dma(sp, sm((cst, cs_pg)), DR(csd), psem)
        ev = (psem[2], psem[0], psem[1], None)
        for pg_ in (cv_pg, pl_pg, fg_pg, h0_pg, misc_pg, cs_pg):
            pg_.w = ev
        for q in range(NQ):
            xsem = self.new_dma_sem("x%d" % q)
            for c in range(8):
                self.dma(sp, xv(c, q), DR(xT[c * 128:(c + 1) * 128, q * QW:(q + 1) * QW]), xsem)
            ev = (xsem[2], xsem[0], xsem[1], None)
            for c in range(8):
                xpg[c][q].w = ev

        pe_, ae, ve, ge = self.pe, self.act, self.dve, self.pool
        ge.deps([], [sm((identf, ident_pg))])
        ins = ge.e.memset(identf[:], 0.0)
        ge.done(ins, [], [sm((identf, ident_pg))])
        ge.deps([], [sm((identf, ident_pg))])
        ins = ge.e.affine_select(out=identf[:], in_=identf[:], pattern=[[-1, 128]], compare_op=ALU.not_equal,
                                 fill=1.0, base=0, channel_multiplier=1)
        ge.done(ins, [], [sm((identf, ident_pg))])
        ge.deps([], [sm((onesb, ones_pg))])
        ins = ge.e.memset(onesb[:], 1.0 / D)
        ge.done(ins, [], [sm((onesb, ones_pg))])
        for (tt_, pg_, val_) in ((epst, eps_pg, EPS), (onet, one_pg, 1.0), (q25t, q25_pg, 0.25), (nst, ns_pg, 0.0)):
            ge.deps([], [sm((tt_, pg_))])
            ins = ge.e.memset(tt_[:], val_)
            ge.done(ins, [], [sm((tt_, pg_))])

        self.activation(sm((csb, csb_pg)), sm((cvt, cv_pg)), AF.Silu)

        def plv(l, off, n):
            return TT(plt[:, l * NP + off:l * NP + off + n], [pl_pg])

        def derv(l, off, n):
            return TT(der[:, l * 160 + off:l * 160 + off + n], [der_pg])

        D_HBA, D_HBX, D_KKH, D_KK, D_NFW, D_SP = 0, 16, 32, 48, 64, 96
        flag = TT(misct[:, 16:17], [misc_pg])
        for l in range(nl):
            self.activation(derv(l, D_SP, 16), plv(l, P_LAM, 16), AF.Exp, scale=-1.0)
        for l in range(nl):
            self.activation(derv(l, D_SP, 16), derv(l, D_SP, 16), AF.Ln, bias=sm((onet, one_pg)))
        for l in range(nl):
            self.v_ts(derv(l, D_KK, 16), derv(l, D_SP, 16), -8.0, None, ALU.mult)
            self.v_ts(derv(l, D_KKH, 16), derv(l, D_SP, 16), -4.0, None, ALU.mult)
            self.v_ts(derv(l, D_HBA, 16), plv(l, P_BA, 16), 0.5, None, ALU.mult)
            self.v_ts(derv(l, D_HBX, 16), plv(l, P_BX, 16), 0.5, None, ALU.mult)
            self.v_ts(derv(l, D_NFW, 32), plv(l, P_CONVW, 32), flag, None, ALU.mult)
            self.v_ts(derv(l, D_NFW, 32), derv(l, D_NFW, 32), -1.0, None, ALU.mult)

        def load_slab(q, s, parts):
            allp = []
            slot_dsem = slot_dsem_sw if q is self.pool else slot_dsem_hw
            for (c0, K, ncols, dap) in parts:
                pgs = spg[s][(c0 * 2) // 1024:(((c0 + K * ncols) * 2 - 1) // 1024) + 1]
                view = slots_b[s][:, c0:c0 + K * ncols].rearrange("p (k n) -> p k n", k=K)
                self.dma(q, TT(view, pgs), DR(dap), slot_dsem[s])
                allp += pgs
            ds = slot_dsem[s]
            for pg_ in allp:
                pg_.w = (ds[2], ds[0], ds[1], None)

        def load_flat(q, s, parts):
            allp = []
            slot_dsem = slot_dsem_sw if q is self.pool else slot_dsem_hw
            for (c0, ncols, dap) in parts:
                pgs = spg[s][(c0 * 2) // 1024:(((c0 + ncols) * 2 - 1) // 1024) + 1]
                self.dma(q, TT(slots_b[s][:, c0:c0 + ncols], pgs), DR(dap), slot_dsem[s])
                allp += pgs
            ds = slot_dsem[s]
            for pg_ in allp:
                pg_.w = (ds[2], ds[0], ds[1], None)

        def wslab(W2d, k0, K, c0, ncols):
            return W2d[k0 * 128:(k0 + K) * 128, c0:c0 + ncols].rearrange("(k p) n -> p k n", p=128)

        class Ring:
            def __init__(self, ids):
                self.ids = ids
                self.i = 0

            def next(self):
                s = self.ids[self.i % len(self.ids)]
                self.i += 1
                return s

        self.ssq_pending = []
        self.ssq_i = 0

        def ssq_accum(b, m, q):
            sq = sm(scrb[self.ssq_i % 3])
            self.ssq_i += 1
            self.activation(sq, xv(m, q), AF.Square)
            self.ssq_pending.append((b, sq, m))

        def ssq_flush(keep):
            while len(self.ssq_pending) > keep:
                b, sq, m = self.ssq_pending.pop(0)
                self.mm(bk(b), sm((onesb, ones_pg)), sq, m == 0, m == 7, force_inc=True)

        def rmsnorm_to(l_sc, l_sh, hslots, rslot, final=False, osem=None, ssq=None):
            def stats(q):
                if ssq is not None:
                    b = ssq[q]
                else:
                    b = bank()
                    for c in range(8):
                        sq = sm(scrb[c % 2])
                        if c % 2 == 0:
                            self.activation(sq, xv(c, q), AF.Square)
                        else:
                            self.v_tt(sq, xv(c, q), xv(c, q), ALU.mult, eng=self.pool)
                        self.mm(bk(b), sm((onesb, ones_pg)), sq, c == 0, c == 7, force_inc=True)
                r = fv(rslot, q * QW, (q + 1) * QW)
                self.activation(r, bk(b), AF.Sqrt, bias=sm((epst, eps_pg)))
                self.reserved.discard(b)
                self.v_recip(r, r)

            def apply(q):
                for c in range(8):
                    r = fv(rslot, q * QW, (q + 1) * QW)
                    if final:
                        self.v_stt(xv(c, q), xv(c, q), TT(fgt[:, c:c + 1], [fg_pg]), r, ALU.mult, ALU.mult)
                        self.dma(sp, DR(yT[c * 128:(c + 1) * 128, q * QW:(q + 1) * QW]), xv(c, q), osem)
                    else:
                        tmp = sm(scr[self.tmp_i % 4])
                        self.tmp_i += 1
                        self.v_tt(tmp, xv(c, q), r, ALU.mult)
                        self.activation(bchunk(hslots[c // 2], c % 2, q), tmp, AF.Identity,
                                        bias=l_sh(c), scale=l_sc(c))
            stats(0)
            stats(1)
            apply(0)
            stats(2)
            apply(1)
            stats(3)
            apply(2)
            apply(3)

        self.tmp_i = 0
        self.ssq_next = None
        self.unit_ctr = 0
        HS = [0, 1, 2, 3]
        RG = [4, 5, 6, 7]
        WS = list(range(8, nslot))

        def hq(k, q):
            return bchunk(HS[k // 2], k % 2, q)

        def proj_chunk(l, col0, q, slab_s, slab_c0, ncols_slab):
            b = bank()
            for k in range(8):
                self.mm(bk(b), slab3(slab_s, 8, ncols_slab, k, slab_c0, slab_c0 + 128), hq(k, q), k == 0, k == 7)
            return b

        def ada_finish(l, bmod, part):
            mod, mod_pg = modt[l]
            sc, sc_pg = sct[l]
            if part in (0, 2):
                self.v_tt(TT(mod[:, 0:16], [mod_pg]), bk(bmod, 0, 16), plv(l, P_ADAB, 16), ALU.add)
                self.v_stt(TT(sc[:, 0:8], [sc_pg]), TT(mod[:, 8:16], [mod_pg]), 1.0, plv(l, P_N1G, 8), ALU.add, ALU.mult)
            if part in (1, 2):
                self.v_tt(TT(mod[:, 16:48], [mod_pg]), bk(bmod, 16, 48), plv(l, P_ADAB + 16, 32), ALU.add)
                self.v_stt(TT(sc[:, 8:16], [sc_pg]), TT(mod[:, 32:40], [mod_pg]), 1.0, plv(l, P_N2G, 8), ALU.add, ALU.mult)

        scr_b = [t.bitcast(BF16) for (t, _) in scr]
        scr_dsem = [self.new_dma_sem("scr%d" % i) for i in range(len(scr))]

        def ada_mini(l, bmod, j0, j1):
            for j in range(j0, j1):
                i = j % len(scr)
                pg = scr[i][1]
                self.dma(pool, TT(scr_b[i][:, 0:1024], [pg]), DR(awd[l * 48 + j]), scr_dsem[i])
                for k in range(8):
                    self.mm(bk(bmod, j, j + 1), TT(scr_b[i][:, k * 128:(k + 1) * 128], [pg]),
                            TT(csb[:, k:k + 1], [csb_pg]), k == 0, k == 7)

        for l in range(nl):
            mod, mod_pg = modt[l]
            sc, sc_pg = sct[l]
            if l == 0:
                ring = Ring(WS[:4])
                bmod = bank()
                self.reserved.add(bmod)
                bmod0 = bmod
                for g in range(4):
                    s = ring.next()
                    load_flat(pool, s, [(jj * 1024, 1024, awd[l * 48 + g * 4 + jj]) for jj in range(4)])
                    for jj in range(4):
                        j = g * 4 + jj
                        for k in range(8):
                            self.mm(bk(bmod, j, j + 1), bv(s, jj * 1024 + k * 128, jj * 1024 + (k + 1) * 128),
                                    TT(csb[:, k:k + 1], [csb_pg]), k == 0, k == 7)
                ada_finish(l, bmod, 0)
            modc = lambda i: TT(mod[:, i:i + 1], [mod_pg])
            rmsnorm_to(lambda c: TT(sc[:, c:c + 1], [sc_pg]), lambda c: modc(c), HS, WS[4], ssq=self.ssq_next)
            self.ssq_next = None

            ring = Ring(WS[:3])
            for g in range(2):
                s = ring.next()
                ncy = QW if g == 0 else 256
                load_slab(pool, s, [(0, 8, ncy, wslab(w_in[l], 0, 8, D_RNN + g * QW, ncy))])
                for q in range(NQ):
                    for jj in range(ncy // 128):
                        c = g * 4 + jj
                        b = proj_chunk(l, 0, q, s, jj * 128, ncy)
                        self.activation(bchunk(RG[c // 2], c % 2, q), bk(b), AF.Gelu_apprx_tanh,
                                        bias=plv(l, P_BIN + 8 + c, 1))

            W_HB, W_A, W_OM, W_IB, W_HF, W_SL = WS[0], WS[2], WS[3], WS[4], WS[5], WS[6]
            if l + 1 < nl:
                bmod_next = bank()
                self.reserved.add(bmod_next)
            if not hasattr(self, "sl_sems"):
                self.sl_sems = [self.new_dma_sem("slx"), self.new_dma_sem("slg")]

            def load_sl_part(parts, dsem):
                allp = []
                for (c0, K, ncols, dap) in parts:
                    pgs = spg[W_SL][(c0 * 2) // 1024:(((c0 + K * ncols) * 2 - 1) // 1024) + 1]
                    view = slots_b[W_SL][:, c0:c0 + K * ncols].rearrange("p (k n) -> p k n", k=K)
                    self.dma(pool, TT(view, pgs), DR(dap), dsem)
                    allp += pgs
                for pg_ in allp:
                    pg_.w = (dsem[2], dsem[0], dsem[1], None)

            def load_sl_x(l_, hd_):
                load_sl_part([(0, 8, 256, wslab(w_in[l_], 0, 8, hd_ * 256, 256))], self.sl_sems[0])

            def load_sl_g(l_, hd_):
                load_sl_part([(2048, 2, 256, wslab(lru_wa[l_, 0, hd_], 0, 2, 0, 256)),
                              (2560, 2, 256, wslab(lru_wx[l_, 0, hd_], 0, 2, 0, 256)),
                              (3072, 2, 256, wslab(lru_wa[l_, 1, hd_], 0, 2, 0, 256)),
                              (3584, 2, 256, wslab(lru_wx[l_, 1, hd_], 0, 2, 0, 256))], self.sl_sems[1])

            XC = [RG[3], WS[1]]

            def prologue_a(hd):
                W_RAW = XC[hd % 2]
                for cc in range(2):
                    c = hd * 2 + cc
                    for q in range(NQ):
                        b = proj_chunk(l, 0, q, W_SL, cc * 128, 256)
                        self.activation(bchunk(W_RAW, cc, q), bk(b), AF.Identity, bias=plv(l, P_BIN + c, 1))
                if hd < 3:
                    load_sl_x(l, hd + 1)
                if hd == 3:
                    load_sl_part([(0, 8, 256, wslab(w_in[l], 0, 8, D_RNN + 768, 256))], self.sl_sems[0])

            def prologue_b(hd):
                W_RAW = XC[hd % 2]
                W_CONV = W_RAW
                for cc in range(2):
                    c = hd * 2 + cc
                    dg, dg_pg = diag[0]
                    for k in range(4):
                        self.v_ts(TT(dg[:, k, :], [dg_pg]), sm((identf, ident_pg)), plv(l, P_CONVW + k * 8 + c, 1), None, ALU.mult)
                    for i, k in enumerate((0, 1, 3)):
                        self.v_ts(TT(dg[:, 4 + i, :], [dg_pg]), sm((identf, ident_pg)), derv(l, D_NFW + k * 8 + c, 1), None, ALU.mult)
                    raw_all = bchunk(W_RAW, cc)
                    rawb = slots_b[W_RAW]
                    base = cc * T

                    def rawv(lo, hi, step=None):
                        if step is None:
                            return TT(rawb[:, base + lo:base + hi], raw_all.pages)
                        return TT(rawb[:, base + lo:base + hi:step], raw_all.pages)

                    cbanks = []
                    for q in range(NQ):
                        b = bank()
                        cbanks.append(b)
                        t0 = q * QW
                        self.mm(bk(b), TT(dg[:, 2, :], [dg_pg]), rawv(t0, t0 + QW), True, False)
                        for k in (0, 1, 3):
                            off = k - 2
                            lo = max(t0, -off)
                            hi = min(t0 + QW, T - off) if off > 0 else t0 + QW
                            hi = min(hi, T)
                            self.mm(bk(b, lo - t0, hi - t0), TT(dg[:, k, :], [dg_pg]), rawv(lo + off, hi + off), False, False)
                        starts = [s0 for s0 in (t0, t0 + SEGW) if s0 > 0]
                        ends = [e for e in (t0 + SEGW - 1, t0 + QW - 1) if e < T - 1]
                        fix = []
                        for s0 in starts:
                            fix.append((s0, 4, s0 - 2))
                            fix.append((s0, 5, s0 - 1))
                            fix.append((s0 + 1, 4, s0 - 1))
                        for e in ends:
                            fix.append((e, 6, e + 1))
                        for i, (oc, di, ic) in enumerate(fix):
                            self.mm(bk(b, oc - t0, oc - t0 + 1), TT(dg[:, di, :], [dg_pg]), rawv(ic, ic + 1), False, i == len(fix) - 1)
                    for q in range(NQ):
                        self.activation(bchunk(W_CONV, cc, q), bk(cbanks[q]), AF.Identity, bias=plv(l, P_CONVB + c, 1))

            def jit_gelu_head3():
                for q in range(NQ):
                    for jj in range(2):
                        c = 6 + jj
                        b = proj_chunk(l, 0, q, W_SL, jj * 128, 256)
                        self.activation(bchunk(RG[3], jj, q), bk(b), AF.Gelu_apprx_tanh,
                                        bias=plv(l, P_BIN + 8 + c, 1))

            for hd in range(4):
                ada_jobs = []
                if l == 0:
                    ada_jobs += [(0, bmod0, j) for j in range(16 + hd * 8, 16 + (hd + 1) * 8)]
                if l + 1 < nl:
                    ada_jobs += [(l + 1, bmod_next, j) for j in range(hd * 12, (hd + 1) * 12)]
                if hd == 0:
                    load_sl_x(l, 0)
                    load_sl_g(l, 0)
                    prologue_a(0)
                    prologue_b(0)
                W_CONV = XC[hd % 2]
                tiny4 = [TT(tiny[i][0][:, 0:4], [tiny[i][1]]) for i in range(4)]
                unit_list = []
                for cc in range(2):
                    for d in range(2):
                        for hv in range(2):
                            unit_list.append((cc, d, hv, hv if d == 0 else 1 - hv))
                for pi in range(4):
                    pair = unit_list[2 * pi:2 * pi + 2]
                    binfo = []
                    if pi == 3 and hd < 3:
                        prologue_a(hd + 1)
                    if pi == 1 and hd == 3:
                        jit_gelu_head3()
                    for ui, (cc, d, hv, th) in enumerate(pair):
                        c = hd * 2 + cc
                        pidx = d * 8 + c
                        wa0 = 2048 + d * 1024
                        wx0 = 2560 + d * 1024
                        k = self.unit_ctr % 2
                        self.unit_ctr += 1
                        binfo.append(k)
                        pr, pi_ = bank_pair(), bank_pair()
                        for qq in range(2):
                            q = 2 * th + qq
                            for kc in range(2):
                                self.mm(bk(2 * pr + qq), bv(W_SL, wa0 + kc * 256 + cc * 128, wa0 + kc * 256 + cc * 128 + 128),
                                        bchunk(W_CONV, kc, q), kc == 0, kc == 1)
                            for kc in range(2):
                                self.mm(bk(2 * pi_ + qq), bv(W_SL, wx0 + kc * 256 + cc * 128, wx0 + kc * 256 + cc * 128 + 128),
                                        bchunk(W_CONV, kc, q), kc == 0, kc == 1)
                        Ak = fv(W_A, k * 1024, (k + 1) * 1024)
                        Ik = fv(W_IB, k * 1024, (k + 1) * 1024)
                        A3 = TT(slots[W_A][:, k * 1024:(k + 1) * 1024].rearrange("p (a b) -> p a b", a=2), Ak.pages)
                        I3 = TT(slots[W_IB][:, k * 1024:(k + 1) * 1024].rearrange("p (a b) -> p a b", a=2), Ik.pages)
                        self.activation(A3, bkpair(pr), AF.Tanh, bias=derv(l, D_HBA + pidx, 1), scale=0.5)
                        self.activation(I3, bkpair(pi_), AF.Tanh, bias=derv(l, D_HBX + pidx, 1), scale=0.5)
                        self.activation(Ak, Ak, AF.Exp, bias=derv(l, D_KKH + pidx, 1), scale=derv(l, D_KKH + pidx, 1))
                    if pi == 3 and hd < 3:
                        load_sl_g(l, hd + 1)
                        prologue_b(hd + 1)
                    nj = (len(ada_jobs) + (3 - pi)) // (4 - pi)
                    for (al, ab, aj) in ada_jobs[:nj]:
                        ada_mini(al, ab, aj, aj + 1)
                    ada_jobs = ada_jobs[nj:]
                    for ui, (cc, d, hv, th) in enumerate(pair):
                        k = binfo[ui]
                        Ak = fv(W_A, k * 1024, (k + 1) * 1024)
                        Ok = fv(W_OM, k * 1024, (k + 1) * 1024)
                        self.v_stt(Ok, Ak, 1.0 - 2e-5, Ak, ALU.min, ALU.mult)
                    for ui, (cc, d, hv, th) in enumerate(pair):
                        k = binfo[ui]
                        Ok = fv(W_OM, k * 1024, (k + 1) * 1024)
                        self.activation(Ok, Ok, AF.Sqrt, bias=sm((q25t, q25_pg)), scale=-0.25)
                    for ui, (cc, d, hv, th) in enumerate(pair):
                        c = hd * 2 + cc
                        k = binfo[ui]
                        Ak = fv(W_A, k * 1024, (k + 1) * 1024)
                        Ok = fv(W_OM, k * 1024, (k + 1) * 1024)
                        Ik = fv(W_IB, k * 1024, (k + 1) * 1024)
                        self.v_stt(Ik, Ik, 1.0, bchunk(W_CONV, cc, None, th * 1024, (th + 1) * 1024), ALU.add, ALU.mult)
                        self.v_tt(Ik, Ik, Ok, ALU.mult)
                        st = 0 if d == 0 else SEGW - 1
                        h0v = TT(h0t[:, (l * 2 + d) * 8 + c:(l * 2 + d) * 8 + c + 1], [h0_pg])
                        kpv = TT(misct[:, d * 8 + 4 * th:d * 8 + 4 * th + 4], [misc_pg])
                        selv = TT(misct[:, 17 + d * 8 + 4 * th:17 + d * 8 + 4 * th + 4], [misc_pg])
                        av = TT(slots[W_A][:, k * 1024 + st:(k + 1) * 1024:SEGW], Ak.pages)
                        ibv = TT(slots[W_IB][:, k * 1024 + st:(k + 1) * 1024:SEGW], Ik.pages)
                        tn = tiny4[self.unit_ctr % 4]
                        if hv == 0:
                            self.v_stt(tn, av, h0v, selv, ALU.mult, ALU.mult)
                            self.v_tt(ibv, ibv, tn, ALU.add)
                        self.v_tt(av, av, kpv, ALU.mult)
                        if d == 0:
                            outv = fv(W_HF, th * 1024, (th + 1) * 1024)
                            ini = None if hv == 0 else TT(slots[W_HF][:, th * 1024 - 1:th * 1024], fv(W_HF, th * 1024 - 1, th * 1024).pages)
                            self.v_scan(outv, Ak, Ik, ini)
                        else:
                            pg_o = fv(W_HB, th * 1024, (th + 1) * 1024).pages
                            outv = TT(slots[W_HB][:, th * 1024:(th + 1) * 1024][:, ::-1], pg_o)
                            ini = None if hv == 0 else TT(slots[W_HB][:, (th + 1) * 1024:(th + 1) * 1024 + 1],
                                                          fv(W_HB, (th + 1) * 1024, (th + 1) * 1024 + 1).pages)
                            self.v_scan(outv, TT(slots[W_A][:, k * 1024:(k + 1) * 1024][:, ::-1], Ak.pages),
                                        TT(slots[W_IB][:, k * 1024:(k + 1) * 1024][:, ::-1], Ik.pages), ini)
                        if hv == 1:
                            hs = W_HF if d == 0 else W_HB
                            fin = SEGW - 1 if d == 0 else 0
                            o = ((l * 2 + d) * 8 + c) * NSEG
                            self.v_copy(TT(nst[:, o:o + NSEG], [ns_pg]), fvs(hs, fin, SEGW))
                    if pi % 2 == 1:
                        cc = pair[0][0]
                        c = hd * 2 + cc
                        for hh_ in range(2):
                            lo, hi = hh_ * 1024, (hh_ + 1) * 1024
                            self.v_tt(fv(W_HF, lo, hi), fv(W_HF, lo, hi), fv(W_HB, lo, hi), ALU.add)
                            self.v_tt(bchunk(RG[c // 2], c % 2, None, lo, hi), fv(W_HF, lo, hi),
                                      bchunk(RG[c // 2], c % 2, None, lo, hi), ALU.mult)
                if hd == 3:
                    if l == 0:
                        ada_finish(0, bmod0, 1)
                        self.reserved.discard(bmod0)
                    if l + 1 < nl:
                        ada_finish(l + 1, bmod_next, 2)
                        self.reserved.discard(bmod_next)

            FO = [WS[0], WS[1]]
            XF, Y0, Y1 = WS[2], WS[3], WS[4]
            tring = Ring(WS[5:])
            for hh in range(2):
                load_slab(pool, tring.ids[0], [(0, 8, 256, wslab(w_in[l], 0, 8, 2 * D_RNN + hh * 256, 256))])
                ws = tring.ids[0]
                tring.i = 1
                for fl in range(2):
                    fc = hh * 2 + fl
                    for q in range(NQ):
                        b = proj_chunk(l, 0, q, ws, fl * 128, 256)
                        self.activation(bchunk(XF, fl, q), bk(b), AF.Identity, bias=plv(l, P_BIN + 16 + fc, 1))
                for tt in range(16):
                    b = bank()
                    for fl in range(2):
                        self.mm(bk(b, fl * 256, (fl + 1) * 256), bchunk(XF, fl, None, tt * 128, (tt + 1) * 128),
                                sm((cst, cs_pg)), True, True)
                    ys = Y0 if tt < 8 else Y1
                    self.v_copy(bv(ys, (tt % 8) * QW, (tt % 8 + 1) * QW), bk(b))
                units = [(sl, hf_) for sl in (tring.ids + [XF]) for hf_ in range(2)]
                if hh == 0:
                    self.unit_i = 2
                if not hasattr(self, "unit_sems"):
                    self.unit_sems = {}
                for u_ in units:
                    if u_ not in self.unit_sems:
                        self.unit_sems[u_] = self.new_dma_sem("unit%d_%d" % u_)
                for pt in range(NQ):
                    bb = [bank(), bank()]
                    for g8 in range(8):
                        sl, hf_ = units[self.unit_i % len(units)]
                        self.unit_i += 1
                        ub = hf_ * 2048
                        uv_ = bv(sl, ub, ub + 2048)
                        self.dma(sp, uv_, DR(tabd[pt * 8 + g8]), self.unit_sems[(sl, hf_)])
                        for fl in range(2):
                            for tl in range(2):
                                tt = g8 * 2 + tl
                                ys = Y0 if tt < 8 else Y1
                                yb = (tt % 8) * QW + fl * 256
                                self.mm(bk(bb[fl]), bv(ys, yb, yb + 128), bv(sl, ub + tl * 1024, ub + tl * 1024 + QW), tt == 0, False)
                                self.mm(bk(bb[fl]), bv(ys, yb + 128, yb + 256), bv(sl, ub + tl * 1024 + QW, ub + tl * 1024 + 2 * QW), False, tt == 15,
                                        force_inc=(tl == 1))
                    for fl in range(2):
                        self.v_copy(bchunk(FO[hh], fl, pt), bk(bb[fl]))

            MGH = [WS[2], WS[3]]
            ring = Ring(WS[4:7])
            ssq2 = [None] * NQ

            def mgv(m, qq):
                col = m * 1024 + qq * QW
                return bv(MGH[col // 4096], col % 4096, col % 4096 + QW)

            for th in range(2):
                for qq in range(2):
                    b_ = bank()
                    self.reserved.add(b_)
                    ssq2[2 * th + qq] = b_
                for m in range(8):
                    s = ring.next()
                    load_flat(pool, s, [(0, 3584, mwd[l * 8 + m])])
                    for qq in range(2):
                        q = 2 * th + qq
                        ba_ = bank()
                        for k in range(8):
                            self.mm(bk(ba_), bv(s, k * 128, (k + 1) * 128), bchunk(RG[k // 2], k % 2, q), k == 0, k == 7)
                        bb_ = bank()
                        for k in range(4):
                            self.mm(bk(bb_), bv(s, 1024 + k * 128, 1024 + (k + 1) * 128), bchunk(FO[k // 2], k % 2, q), k == 0, k == 3)
                        bga = bank()
                        for k in range(8):
                            self.mm(bk(bga), bv(s, 1536 + k * 128, 1536 + (k + 1) * 128), hq(k, q), k == 0, k == 7)
                        bgb = bank()
                        for k in range(8):
                            self.mm(bk(bgb), bv(s, 2560 + k * 128, 2560 + (k + 1) * 128), hq(k, q), k == 0, k == 7)
                        ga = sm(scr[(qq % 2) * 2])
                        gb = sm(scr[1 + (qq % 2) * 2])
                        self.activation(ga, bk(bga), AF.Sigmoid, bias=plv(l, P_BIN + 20 + m, 1))
                        self.activation(gb, bk(bgb), AF.Sigmoid, bias=plv(l, P_BIN + 28 + m, 1))
                        self.v_tt(ga, ga, bk(ba_), ALU.mult)
                        self.v_tt(gb, gb, bk(bb_), ALU.mult)
                        self.v_tt(mgv(m, qq), ga, gb, ALU.add)
                for g in range(2):
                    s = ring.next()
                    load_slab(pool, s, [(0, 8, QW, wslab(w_out[l], 0, 8, g * QW, QW))])
                    for jj in range(4):
                        m = g * 4 + jj
                        for qq in range(2):
                            q = 2 * th + qq
                            b = bank()
                            for k in range(8):
                                self.mm(bk(b), slab3(s, 8, QW, k, jj * 128, (jj + 1) * 128), mgv(k, qq), k == 0, k == 7)
                            ssq_flush(2)
                            self.v_stt(xv(m, q), bk(b), modc(16 + m), xv(m, q), ALU.mult, ALU.add)
                            ssq_accum(ssq2[q], m, q)
                ssq_flush(0)

            rmsnorm_to(lambda c: TT(sc[:, 8 + c:9 + c], [sc_pg]), lambda c: modc(24 + c), HS, WS[6], ssq=ssq2)

            ACTS = RG + WS[2:4]
            ring = Ring(WS[0:2] + WS[4:])
            for hf in range(2):
                for g in range(6):
                    ncg = QW if g < 5 else 256
                    su = ring.next()
                    load_slab(pool, su, [(0, 8, ncg, wslab(ffn_w_in[l], 0, 8, g * QW, ncg))])
                    sv_ = ring.next()
                    load_slab(pool, sv_, [(0, 8, ncg, wslab(ffn_w_in[l], 0, 8, D_FF + g * QW, ncg))])
                    for qq in range(2):
                        q = hf * 2 + qq
                        for jj in range(ncg // 128):
                            j = g * 4 + jj
                            bu = bank()
                            for k in range(8):
                                self.mm(bk(bu), slab3(su, 8, ncg, k, jj * 128, (jj + 1) * 128), hq(k, q), k == 0, k == 7)
                            bv_ = bank()
                            for k in range(8):
                                self.mm(bk(bv_), slab3(sv_, 8, ncg, k, jj * 128, (jj + 1) * 128), hq(k, q), k == 0, k == 7)
                            su_t = sm(scr[2 + (jj % 2)])
                            self.activation(su_t, bk(bu), AF.Silu)
                            col = j * 1024 + qq * QW
                            self.v_tt(bv(ACTS[col // 4096], col % 4096, col % 4096 + QW), su_t, bk(bv_), ALU.mult)
                if hf == 0:
                    self.ssq_next = [None] * NQ
                for qq in range(2):
                    b_ = bank()
                    self.reserved.add(b_)
                    self.ssq_next[hf * 2 + qq] = b_
                for mp in range(4):
                    s0 = ring.next()
                    load_slab(pool, s0, [(0, 11, 256, wslab(ffn_w_out[l], 0, 11, mp * 256, 256))])
                    s1 = ring.next()
                    load_slab(pool, s1, [(0, 11, 256, wslab(ffn_w_out[l], 11, 11, mp * 256, 256))])
                    for mm_ in range(2):
                        m = mp * 2 + mm_
                        for qq in range(2):
                            q = hf * 2 + qq
                            b = bank()
                            for j in range(NJ):
                                ss = s0 if j < 11 else s1
                                col = j * 1024 + qq * QW
                                self.mm(bk(b), slab3(ss, 11, 256, j % 11, mm_ * 128, (mm_ + 1) * 128),
                                        bv(ACTS[col // 4096], col % 4096, col % 4096 + QW), j == 0, j == NJ - 1)
                            ssq_flush(1)
                            self.v_stt(xv(m, q), bk(b), modc(40 + m), xv(m, q), ALU.mult, ALU.add)
                            ssq_accum(self.ssq_next[q], m, q)
                ssq_flush(0)

        osem = self.new_dma_sem("out")
        rmsnorm_to(None, None, None, WS[0], final=True, osem=osem, ssq=self.ssq_next)
        self.dma(sp, DR(nsd), sm((nst, ns_pg)), osem)
        sp.e.wait_ge(osem[0], osem[1])
        return nc


_CACHE = {}


def _dft_tables():
    if "t" in _CACHE:
        return _CACHE["t"]
    bf = ml_dtypes.bfloat16
    ch = np.arange(64)
    th = 2 * np.pi * np.outer(ch, ch) / 64.0
    cs = np.zeros((128, 256), np.float64)
    for g in range(2):
        cs[g * 64:(g + 1) * 64, g * 64:(g + 1) * 64] = np.cos(th) / 8.0
        cs[g * 64:(g + 1) * 64, 128 + g * 64:128 + (g + 1) * 64] = np.sin(th) / 8.0
    p = np.arange(T)
    r, c = p // 64, p % 64
    ths = 2 * np.pi * (np.outer(r, r) / 32.0 + np.outer(c, c) / 64.0)
    cp_s = (np.cos(ths) / np.sqrt(T)).astype(np.float32).astype(bf)
    nsp_s = (-np.sin(ths) / np.sqrt(T)).astype(np.float32).astype(bf)
    q = np.arange(SEGW)
    thp = 2 * np.pi * np.outer(q, q) / float(SEGW)
    cp_p = np.zeros((T, T), np.float32)
    nsp_p = np.zeros((T, T), np.float32)
    for s in range(NSEG):
        cp_p[s * SEGW:(s + 1) * SEGW, s * SEGW:(s + 1) * SEGW] = np.cos(thp) / 16.0
        nsp_p[s * SEGW:(s + 1) * SEGW, s * SEGW:(s + 1) * SEGW] = -np.sin(thp) / 16.0
    def units(cp, nsp):
        a = np.stack([np.asarray(cp), np.asarray(nsp)], 0).reshape(2, 8, 2, 128, 4, 512)
        return np.ascontiguousarray(np.transpose(a, (4, 1, 3, 2, 0, 5))).reshape(32, 128, 2048)
    _CACHE["t"] = (cs.astype(np.float32).astype(bf), units(cp_s, nsp_s), units(cp_p.astype(bf), nsp_p.astype(bf)))
    return _CACHE["t"]


def _pc(v):
    v = np.asarray(v, np.float32)
    sh = v.shape[:-1]
    return np.moveaxis(v.reshape(sh + (8, 128)), -1, 0)


def kernel(x_prompt, x_sample, state_lru, c, c_ctx, norm1_g, norm2_g, ada_w, ada_b, w_in, b_in, conv_w, conv_b,
           lru_wa, lru_ba, lru_wx, lru_bx, lru_lambda, w_lru_out, w_fnet_out, w_out, ffn_w_in, ffn_w_out, final_g):
    n_layers = int(os.environ.get("MK_LAYERS", L))
    f32 = lambda a: np.ascontiguousarray(np.asarray(a, np.float32))
    cs, tab_s, tab_p = _dft_tables()
    plp = np.zeros((128, L, NP), np.float32)
    plp[:, :, P_N1G:P_N1G + 8] = _pc(norm1_g)
    plp[:, :, P_N2G:P_N2G + 8] = _pc(norm2_g)
    plp[:, :, P_ADAB:P_ADAB + 48] = np.moveaxis(f32(ada_b).reshape(L, 48, 128), -1, 0)
    plp[:, :, P_BIN:P_BIN + 36] = np.moveaxis(f32(b_in).reshape(L, 36, 128), -1, 0)
    plp[:, :, P_CONVW:P_CONVW + 32] = _pc(conv_w).reshape(128, L, 32)
    plp[:, :, P_CONVB:P_CONVB + 8] = _pc(conv_b)
    plp[:, :, P_BA:P_BA + 16] = _pc(lru_ba).reshape(128, L, 16)
    plp[:, :, P_BX:P_BX + 16] = _pc(lru_bx).reshape(128, L, 16)
    plp[:, :, P_LAM:P_LAM + 16] = _pc(lru_lambda).reshape(128, L, 16)
    plp = np.ascontiguousarray(plp.reshape(128, L * NP))
    fgp = np.ascontiguousarray(_pc(final_g))
    ada_w = f32(ada_w)
    w_in = f32(w_in)
    aw = np.ascontiguousarray(np.transpose(ada_w.reshape(L, 8, 128, 48, 128), (0, 3, 2, 1, 4))).reshape(L * 48, 128, 1024)
    def blk(W, c0):
        K = W.shape[1] // 128
        return np.transpose(W[:, :, c0:c0 + 1024].reshape(L, K, 128, 8, 128), (0, 3, 2, 1, 4))
    mw = np.concatenate([blk(f32(w_lru_out), 0), blk(f32(w_fnet_out), 0), blk(w_in, 2 * D_RNN + D_FNET),
                         blk(w_in, 2 * D_RNN + D_FNET + D)], axis=3)
    mw = np.ascontiguousarray(mw).reshape(L * 8, 128, 3584)
    shared = {"pl": plp, "fg": fgp, "aw": aw, "mw": mw, "w_in": w_in, "lru_wa": f32(lru_wa), "lru_wx": f32(lru_wx),
              "w_out": f32(w_out),
              "ffn_w_in": f32(ffn_w_in), "ffn_w_out": f32(ffn_w_out), "cs": cs}
    x_prompt = f32(x_prompt)
    x_sample = f32(x_sample)
    state_lru = f32(state_lru)
    in_maps = []
    for core in range(8):
        m = dict(shared)
        misc = np.zeros((128, 33), np.float32)
        h0 = np.zeros((128, L, 2, 8), np.float32)
        if core < 4:
            xin = x_sample[core]
            cvec = f32(c)[core]
            m["tab"] = tab_s
            misc[:, 0:8] = 1.0
            misc[:, 0] = 0.0
            misc[:, 8:16] = 1.0
            misc[:, 15] = 0.0
            h0 = _pc(state_lru[core])
            misc[:, 17] = 1.0
            misc[:, 17 + 8 + NSEG - 1] = 1.0
        else:
            pi = (core - 4) % 2
            xin = x_prompt[pi * 8:(pi + 1) * 8].reshape(T, D)
            cvec = f32(c_ctx)
            m["tab"] = tab_p
            misc[:, 16] = 1.0
        m["xT"] = np.ascontiguousarray(xin.T)
        m["cv"] = np.ascontiguousarray(cvec.reshape(8, 128).T)
        m["h0"] = np.ascontiguousarray(h0.reshape(128, -1))
        m["misc"] = misc
        in_maps.append(m)
    key = ("nc", n_layers)
    if key not in _CACHE:
        _CACHE[key] = Builder(n_layers).build()
    nc = _CACHE[key]
    res = run_bass_kernel_spmd(nc, in_maps, core_ids=list(range(8)))
    outs = res.results
    y_sample = np.stack([np.ascontiguousarray(outs[b]["yT"].T) for b in range(4)], 0)
    y_prompt = np.concatenate([np.ascontiguousarray(outs[4 + i]["yT"].T).reshape(8, SEGW, D) for i in range(2)], 0)
    ns = np.zeros((16, L, 2, D_RNN), np.float32)
    for i in range(2):
        a = outs[4 + i]["ns"].reshape(128, L, 2, 8, NSEG)
        ns[i * 8:(i + 1) * 8] = np.transpose(a, (4, 1, 2, 3, 0)).reshape(NSEG, L, 2, D_RNN)
    return (y_prompt.astype(np.float32), y_sample.astype(np.float32), ns)
```
